# Optimizing a Trainium2 kernel written in Bass

```python
import math
import jax, jax.numpy as jnp
from jax import lax
import numpy as np

D_MODEL = 1024
BATCH = 4
SEQ = 4096
DEPTH = 4

CTX_LEN = 256
GRID_W = 64
POOL_WIDTH = 512
POOL_WINDOWS = (2, 4, 8, 16)
POOL_GROUPS = len(POOL_WINDOWS)
POOL_GC = POOL_WIDTH // POOL_GROUPS
N_HEADS = 4
HEAD_DIM = 64
V_DIM = 2 * HEAD_DIM
QK_WIDTH = N_HEADS * 2 * HEAD_DIM
ATTN_WIDTH = N_HEADS * V_DIM
MIX_WIDTH = POOL_WIDTH + ATTN_WIDTH
IN_WIDTH = POOL_WIDTH + 2 * QK_WIDTH + ATTN_WIDTH
ROT_AXIS = HEAD_DIM // 2
ROPE_BASE = 10000.0
D_FF = 4 * D_MODEL
Q_BLOCK = 128
N_ADA = 6
EPS = 1e-6

kernel_name = "hybrid_pool_diffattn_dit_block"


def rms_norm(x, g):
    xf = x.astype(jnp.float32)
    y = xf * lax.rsqrt(jnp.mean(xf * xf, axis=-1, keepdims=True) + EPS)
    return (y * g.astype(jnp.float32)).astype(x.dtype)


def modulate(h, shift, scale):
    return h * (1 + scale) + shift


def ada_params(cvec, w_ada, b_ada):
    return jnp.split(jax.nn.silu(cvec) @ w_ada + b_ada, N_ADA, axis=-1)


def axial_rope(n, dtype):
    rows = n // GRID_W
    row = jnp.repeat(jnp.arange(rows), GRID_W).astype(jnp.float32)
    col = jnp.tile(jnp.arange(GRID_W), rows).astype(jnp.float32)
    inv = ROPE_BASE ** (-jnp.arange(0, ROT_AXIS, 2, dtype=jnp.float32) / ROT_AXIS)
    ang = jnp.concatenate([row[:, None] * inv, col[:, None] * inv], axis=-1)
    return jnp.cos(ang).astype(dtype), jnp.sin(ang).astype(dtype)


def apply_rope(x, cos, sin):
    x1, x2 = x[..., :HEAD_DIM // 2], x[..., HEAD_DIM // 2:]
    return jnp.concatenate([x1 * cos - x2 * sin, x1 * sin + x2 * cos], axis=-1)


def centred_mean(x, w):
    b, n, ch = x.shape
    xf = x.astype(jnp.float32)
    cs = jnp.concatenate([jnp.zeros((b, 1, ch), jnp.float32), jnp.cumsum(xf, axis=1)], axis=1)
    t = jnp.arange(n)
    lo = jnp.clip(t - w // 2, 0, n)
    hi = jnp.clip(t + w - w // 2, 0, n)
    s = cs[:, hi] - cs[:, lo]
    cnt = (hi - lo).astype(jnp.float32)
    return (s / cnt[None, :, None]).astype(x.dtype)


def pool_mixer(u, w_pool, s_pool):
    b, n, _ = u.shape
    groups = jnp.split(u, POOL_GROUPS, axis=-1)
    y = jnp.stack([centred_mean(g, w) - g for g, w in zip(groups, POOL_WINDOWS)], axis=2)
    y = jnp.einsum('bngc,gcd->bngd', y, w_pool).reshape(b, n, POOL_WIDTH)
    return y * s_pool


def split_heads(z):
    b, n, _ = z.shape
    u, q, k, v = jnp.split(z, [POOL_WIDTH, POOL_WIDTH + QK_WIDTH, POOL_WIDTH + 2 * QK_WIDTH], axis=-1)
    q = q.reshape(b, n, 2 * N_HEADS, HEAD_DIM).transpose(0, 2, 1, 3)
    k = k.reshape(b, n, 2 * N_HEADS, HEAD_DIM).transpose(0, 2, 1, 3)
    v = v.reshape(b, n, N_HEADS, V_DIM).transpose(0, 2, 1, 3)
    return u, q, k, v


def diff_attn(q, k, v, lam):
    b, _, nq, _ = q.shape
    nk = k.shape[2]
    s = jnp.einsum('bhqd,bhkd->bhqk', q, k).astype(jnp.float32) * (HEAD_DIM ** -0.5)
    a = jax.nn.softmax(s, axis=-1).reshape(b, N_HEADS, 2, nq, nk)
    w = a[:, :, 0] - lam * a[:, :, 1]
    return jnp.einsum('bhqk,bhkd->bhqd', w.astype(v.dtype), v)


def diff_attn_blocks(q, k, v, lam):
    b, hh, n, d = q.shape
    nb = n // Q_BLOCK
    qb = q.reshape(b, hh, nb, Q_BLOCK, d).transpose(2, 0, 1, 3, 4)
    o = lax.map(lambda qblk: diff_attn(qblk, k, v, lam), qb)
    return o.transpose(1, 2, 0, 3, 4).reshape(b, N_HEADS, n, V_DIM)


def head_out(o, g_sub, lam_init):
    o = rms_norm(o, g_sub) * (1.0 - lam_init)
    b, h, n, d = o.shape
    return o.transpose(0, 2, 1, 3).reshape(b, n, h * d)


def sq_relu_mlp(h, w1, w2):
    return jnp.square(jax.nn.relu(h @ w1)) @ w2


def setup_inputs(seed: int = 0) -> dict:
    key = jax.random.key(seed)
    ks = jax.random.split(key, 24)
    f = jnp.float32
    nrm = lambda k, shape, s: jax.random.normal(k, shape, f) * s
    L = DEPTH
    return {
        "x": nrm(ks[0], (BATCH, SEQ, D_MODEL), 1.0),
        "c": nrm(ks[1], (BATCH, D_MODEL), 1.0),
        "ctx": nrm(ks[2], (BATCH, CTX_LEN, D_MODEL), 1.0),
        "c_ctx": nrm(ks[3], (D_MODEL,), 1.0),
        "w_ada": nrm(ks[4], (L, D_MODEL, N_ADA * D_MODEL), 0.02),
        "b_ada": nrm(ks[5], (L, N_ADA * D_MODEL), 0.01),
        "g_mix": 1.0 + nrm(ks[6], (L, D_MODEL), 0.02),
        "g_mlp": 1.0 + nrm(ks[7], (L, D_MODEL), 0.02),
        "w_in": nrm(ks[8], (L, D_MODEL, IN_WIDTH), D_MODEL ** -0.5),
        "w_pool": nrm(ks[9], (L, POOL_GROUPS, POOL_GC, POOL_GC), POOL_GC ** -0.5),
        "s_pool": 1.0 + nrm(ks[10], (L, POOL_WIDTH), 0.1),
        "lam_q1": nrm(ks[11], (L, HEAD_DIM), 0.1),
        "lam_k1": nrm(ks[12], (L, HEAD_DIM), 0.1),
        "lam_q2": nrm(ks[13], (L, HEAD_DIM), 0.1),
        "lam_k2": nrm(ks[14], (L, HEAD_DIM), 0.1),
        "g_subln": 1.0 + nrm(ks[15], (L, V_DIM), 0.02),
        "w_out": nrm(ks[16], (L, MIX_WIDTH, D_MODEL), MIX_WIDTH ** -0.5),
        "w_mlp1": nrm(ks[17], (L, D_MODEL, D_FF), D_MODEL ** -0.5),
        "w_mlp2": nrm(ks[18], (L, D_FF, D_MODEL), D_FF ** -0.5),
        "g_final": 1.0 + nrm(ks[19], (D_MODEL,), 0.02),
    }


def reference(x, c, ctx, c_ctx, w_ada, b_ada, g_mix, g_mlp, w_in, w_pool, s_pool,
              lam_q1, lam_k1, lam_q2, lam_k2, g_subln, w_out, w_mlp1, w_mlp2, g_final):
    n = x.shape[1]
    cos, sin = axial_rope(n, x.dtype)
    c_lat = c[:, None, :]
    c_cx = c_ctx[None, None, :]
    for l in range(DEPTH):
        last = l == DEPTH - 1
        sa, ca, ga, sm, cm, gm = ada_params(c_lat, w_ada[l], b_ada[l])
        sa_c, ca_c, ga_c, sm_c, cm_c, gm_c = ada_params(c_cx, w_ada[l], b_ada[l])
        lam_init = 0.8 - 0.6 * math.exp(-0.3 * l)
        lam = (jnp.exp(jnp.sum(lam_q1[l].astype(jnp.float32) * lam_k1[l].astype(jnp.float32)))
               - jnp.exp(jnp.sum(lam_q2[l].astype(jnp.float32) * lam_k2[l].astype(jnp.float32)))
               + lam_init)

        h = modulate(rms_norm(x, g_mix[l]), sa, ca)
        hc = modulate(rms_norm(ctx, g_mix[l]), sa_c, ca_c)
        u, q, k, v = split_heads(h @ w_in[l])
        uc, qc, kc, vc = split_heads(hc @ w_in[l])
        q = apply_rope(q, cos, sin)
        k = apply_rope(k, cos, sin)
        k_all = jnp.concatenate([kc, k], axis=2)
        v_all = jnp.concatenate([vc, v], axis=2)
        o_att = diff_attn_blocks(q, k_all, v_all, lam)
        mix = jnp.concatenate([pool_mixer(u, w_pool[l], s_pool[l]),
                               head_out(o_att, g_subln[l], lam_init)], axis=-1)
        x = x + ga * (mix @ w_out[l])

        x = x + gm * sq_relu_mlp(modulate(rms_norm(x, g_mlp[l]), sm, cm), w_mlp1[l], w_mlp2[l])

        if not last:
            o_c = diff_attn(qc, kc, vc, lam)
            mix_c = jnp.concatenate([pool_mixer(uc, w_pool[l], s_pool[l]),
                                     head_out(o_c, g_subln[l], lam_init)], axis=-1)
            ctx = ctx + ga_c * (mix_c @ w_out[l])
            ctx = ctx + gm_c * sq_relu_mlp(modulate(rms_norm(ctx, g_mlp[l]), sm_c, cm_c),
                                           w_mlp1[l], w_mlp2[l])
    return rms_norm(x, g_final)
```

```python
import contextlib
import math
import numpy as np
import concourse.bass as bass
import concourse.mybir as mybir
from concourse.bass_utils import run_bass_kernel_spmd

F32 = mybir.dt.float32
BF16 = mybir.dt.bfloat16
AF = mybir.ActivationFunctionType
ALU = mybir.AluOpType
AX = mybir.AxisListType

DEPTH = 4
D = 1024
KC = 8
TL = 2176
NT = 17
NKT = 34
EPS = 1e-6
SEND_ROWS = 4384
BLK = [(0, 128)] + [(128 + 512 * i, 512) for i in range(4)]
UCOLS = 2208
ENGS = ("pe", "act", "dve", "pool", "sp")
EPOCH = 8000


def ucol(tl):
    return tl + 8 if tl < 128 else tl + 24


class Buf:
    __slots__ = ("name", "w", "r")

    def __init__(self, name=""):
        self.name = name
        self.w = []
        self.r = []

    def inherit(self, *others):
        for o in others:
            self.r.extend(o.w)
            self.r.extend(o.r)
        return self


class Sched:
    def __init__(self, nc, n_dma_sems=32):
        self.nc = nc
        self.lists = {e: [] for e in ENGS}
        self.cnt = {e: 0 for e in ENGS}
        self.known = {e: {} for e in ENGS}
        self.n_dma_sems = n_dma_sems
        self.dma_val = [0] * n_dma_sems
        self.dma_rr = 0
        self.n_cc_sems = 12
        self.cc_val = [0] * 12
        self.cc_rr = 0
        self.sems = {}

    @staticmethod
    def _ekey(eng, n):
        ep = (n - 1) // EPOCH
        return ((eng, ep), n - ep * EPOCH)

    def _need(self, eng, ev, waits):
        if ev is None:
            return
        k, v = ev
        if self.known[eng].get(k, 0) >= v:
            return
        if eng == "pe" and k[0] == "pe":
            return
        waits[k] = max(waits.get(k, 0), v)

    def _collect(self, eng, reads, writes, acc=False):
        waits = {}
        for b in reads:
            for ev in b.w:
                self._need(eng, ev, waits)
        for b in writes:
            if not acc:
                for ev in b.w:
                    self._need(eng, ev, waits)
            for ev in b.r:
                self._need(eng, ev, waits)
        for k, v in waits.items():
            self.known[eng][k] = v
        return list(waits.items())

    def _record(self, ev, reads, writes, acc=False):
        for b in reads:
            b.r.append(ev)
        for b in writes:
            if acc and not b.r:
                b.w.append(ev)
            else:
                b.w = [ev]
                b.r = []

    def op(self, eng, fn, reads=(), writes=(), inc=True):
        waits = self._collect(eng, reads, writes)
        if inc:
            self.cnt[eng] += 1
            ev = self._ekey(eng, self.cnt[eng])
        else:
            ev = self._ekey(eng, self.cnt[eng] + 1)
        self.lists[eng].append(("op", waits, fn, ev if inc else None))
        self._record(ev, reads, writes)
        return ev

    def dma(self, q, fn, reads=(), writes=(), acc=True):
        waits = self._collect(q, reads, writes, acc)
        i = self.dma_rr
        self.dma_rr = (self.dma_rr + 1) % self.n_dma_sems
        key = ("dma", i)
        prev = self.dma_val[i]
        if prev and self.known[q].get(key, 0) < prev:
            waits.append((key, prev))
            self.known[q][key] = prev
        self.dma_val[i] = prev + 16
        ev = (key, prev + 16)
        self.lists[q].append(("dma", waits, fn, key))
        self._record(ev, reads, writes, acc)
        return ev

    def coll(self, fn, reads=(), writes=()):
        q = "pool"
        waits = self._collect(q, reads, writes)
        i = self.cc_rr
        self.cc_rr = (self.cc_rr + 1) % self.n_cc_sems
        key = ("cc", i)
        prev = self.cc_val[i]
        if prev and self.known[q].get(key, 0) < prev:
            waits.append((key, prev))
            self.known[q][key] = prev
        self.cc_val[i] = prev + 1
        ev = (key, prev + 1)
        self.lists[q].append(("cc", waits, fn, key))
        self._record(ev, reads, writes)
        return ev

    def wait_all(self, eng, bufs):
        waits = self._collect(eng, [], bufs)
        self.lists[eng].append(("wait", waits, None, None))

    def _emit(self, ename, eng):
        sems = self.sems
        for kind, waits, fn, x in self.lists[ename]:
            for k, v in waits:
                eng.wait_ge(sems[k], v)
            if kind == "op":
                ins = fn(eng)
                if x is not None:
                    ins.then_inc(sems[x[0]], 1)
            elif kind == "dma":
                fn(eng).then_inc(sems[x], 16)
            elif kind == "cc":
                fn(eng).then_inc(sems[x], 1)

    def run(self, stack):
        nc = self.nc
        for e in ENGS:
            nep = (max(self.cnt[e], 1) - 1) // EPOCH + 1
            for ep in range(nep + 1):
                self.sems[(e, ep)] = stack.enter_context(nc.semaphore("s_%s%d" % (e, ep)))
        for i in range(self.n_dma_sems):
            self.sems[("dma", i)] = stack.enter_context(nc.semaphore("s_dma%d" % i))
        for i in range(self.n_cc_sems):
            self.sems[("cc", i)] = stack.enter_context(nc.semaphore("s_cc%d" % i))
        block = stack.enter_context(nc.Block())
        block.tensor(lambda eng: self._emit("pe", eng))
        block.scalar(lambda eng: self._emit("act", eng))
        block.vector(lambda eng: self._emit("dve", eng))
        block.gpsimd(lambda eng: self._emit("pool", eng))
        block.sync(lambda eng: self._emit("sp", eng))


class Rot:
    def __init__(self, items):
        self.items = items
        self.i = 0

    def get(self):
        it = self.items[self.i % len(self.items)]
        self.i += 1
        return it


class _Stop(Exception):
    pass


def build_program(n_layers=DEPTH, dump=None, stop=None):
    def stage(k):
        if stop is not None and k == stop:
            raise _Stop()

    nc = bass.Bass("TRN2", target_bir_lowering=False)

    def din(name, shape, dt=F32):
        return nc.dram_tensor(name, list(shape), dt, kind="ExternalInput").ap()

    x_loc = din("x_loc", [TL, D])
    c5T_d = din("c2T", [128, KC, 2])
    w_ada_d = din("w_ada_sl", [DEPTH, D, 3072])
    b_ada_d = din("b_adaT", [128, DEPTH, 48])
    g_mix_d = din("g_mixT", [128, DEPTH, KC])
    g_mlp_d = din("g_mlpT", [128, DEPTH, KC])
    g_fin_d = din("g_finT", [128, KC])
    s_pool_d = din("s_poolT", [128, DEPTH, 4])
    g_sub_d = din("g_subT", [128, DEPTH])
    lamv_d = din("lamv", [128, 4, DEPTH, 64])
    w_in_d = din("w_in", [n_layers, D, 2048])
    w_pool_d = din("w_pool", [n_layers, 4, 128, 128])
    w_out_d = din("w_out", [n_layers, D, D])
    w1_d = din("w_mlp1", [n_layers, D, 4096])
    w2_d = din("w_mlp2", [n_layers, 4096, D])
    cos_d = din("cosT", [128, 2048])
    sin_d = din("sinT", [128, 2048])
    invc_d = din("invc", [128, 4, 4, 8])
    msk_d = din("msk", [128, 2])
    ident_d = din("ident", [128, 128])
    pmat_d = din("pmat", [128, 128])
    out_d = nc.dram_tensor("out", [2048, D], F32, kind="ExternalOutput").ap()
    dump_d = None
    if dump is not None:
        dump_d = nc.dram_tensor("dump", [128, KC, TL], F32, kind="ExternalOutput").ap()

    send = nc.dram_tensor("send", [SEND_ROWS, 512], BF16)
    recv = nc.dram_tensor("recv", [2 * SEND_ROWS, 512], BF16)
    ada_send = nc.dram_tensor("ada_send", [128, 192], F32)
    ada_recv = nc.dram_tensor("ada_recv", [256, 192], F32)

    S = Sched(nc)
    with contextlib.ExitStack() as st:
        def T(name, shape, dt):
            return st.enter_context(nc.sbuf_tensor("sb_" + name, list(shape), dt))

        xT = T("xT", [128, KC, TL], F32)
        aT = T("aT", [128, KC, TL], BF16)
        qT = T("qT", [128, 4, TL], BF16)
        uT = T("uT", [128, 4, UCOLS], BF16)
        ident = T("ident", [128, 128], F32)
        pmat = T("pmat", [128, 128], F32)
        ones_bf = T("ones_bf", [128, 128], BF16)
        pmat_bf = T("pmat_bf", [128, 128], BF16)
        ada_c = T("ada_c", [128, DEPTH, 48], F32)
        ada_l = T("ada_l", [128, DEPTH, 48], F32)
        gc = T("gc", [128, 2, 2, DEPTH, KC], F32)
        g_mix = T("g_mix", [128, DEPTH, KC], F32)
        g_mlp = T("g_mlp", [128, DEPTH, KC], F32)
        g_fin = T("g_fin", [128, KC], F32)
        s_pool = T("s_pool", [128, DEPTH, 4], F32)
        g_sub = T("g_sub", [128, DEPTH], F32)
        nlam = T("nlam", [128, DEPTH], F32)
        invc = T("invc", [128, 4, 4, 8], F32)
        msk = T("msk", [128, 2], F32)
        ARENA = 67584
        arena = T("arena", [128, ARENA // 2], BF16)

        psb = [st.enter_context(nc.psum_tensor("ps%d" % i, [128, 512], F32)) for i in range(8)]
        ps_all = [(psb[i], Buf("ps%d" % i)) for i in range(8)]

        A = {"off": 0, "bufs": [], "old": Buf("old")}

        def phase():
            old = Buf("old")
            old.inherit(A["old"], *A["bufs"])
            A["old"] = old
            A["bufs"] = []
            A["off"] = 0

        def aalloc(shape, dt, name=""):
            n = int(np.prod(shape))
            nb = n * (2 if dt == BF16 else 4)
            nb_al = (nb + 63) // 64 * 64
            off = A["off"]
            assert off + nb_al <= ARENA, (name, off, nb_al)
            A["off"] = off + nb_al
            v = arena[:, off // 2: off // 2 + nb // 2]
            if dt == F32:
                v = v.bitcast(F32)
            if len(shape) == 2:
                v = v.rearrange("p (a b) -> p a b", b=shape[1])
            elif len(shape) == 3:
                v = v.rearrange("p (a b c) -> p a b c", b=shape[1], c=shape[2])
            b = Buf(name).inherit(A["old"])
            A["bufs"].append(b)
            return v, b

        def apool(n, shape, dt, name=""):
            return Rot([aalloc(shape, dt, "%s%d" % (name, i)) for i in range(n)])

        def mm(out, lhsT, rhs, start, stop):
            return lambda e: e.matmul(out, lhsT=lhsT, rhs=rhs, start=start, stop=stop)

        def act(out, in_, func, bias=None, scale=None):
            kw = {}
            if bias is not None:
                kw["bias"] = bias
            if scale is not None:
                kw["scale"] = scale
            return lambda e: e.activation(out=out, in_=in_, func=func, **kw)

        def tt(out, in0, in1, op):
            return lambda e: e.tensor_tensor(out=out, in0=in0, in1=in1, op=op)

        def stt(out, in0, scalar, in1, op0, op1):
            return lambda e: e.scalar_tensor_tensor(out=out, in0=in0, scalar=scalar, in1=in1, op0=op0, op1=op1)

        def ts(out, in0, s1, s2, op0, op1=None):
            if op1 is None:
                return lambda e: e.tensor_scalar(out=out, in0=in0, scalar1=s1, scalar2=None, op0=op0)
            return lambda e: e.tensor_scalar(out=out, in0=in0, scalar1=s1, scalar2=s2, op0=op0, op1=op1)

        def cp(out, in_):
            return lambda e: e.tensor_copy(out=out, in_=in_)

        def dm(out, in_):
            return lambda e: e.dma_start(out=out, in_=in_)

        b_x = [[Buf("x%d_%d" % (i, k)) for k in range(KC)] for i in range(5)]
        b_a = [[Buf("a%d_%d" % (i, k)) for k in range(KC)] for i in range(5)]
        b_q = [[Buf("q%d_%d" % (i, h)) for h in range(4)] for i in range(5)]
        b_u = [[Buf("u%d_%d" % (g, i)) for i in range(5)] for g in range(4)]
        b_uh = Buf("uhalo")
        b_const = Buf("const")
        b_ada = Buf("ada")
        b_send_k = [Buf("send_k%d" % h) for h in range(4)]
        b_send_v = [Buf("send_v%d" % h) for h in range(4)]
        b_send_u = Buf("send_u")
        b_rk = [Buf("rk%d" % h) for h in range(4)]
        b_rv = [Buf("rv%d" % h) for h in range(4)]
        b_ru = Buf("ru")

        b_out = Buf("out")
        try:
            for dst, src in ((ident, ident_d), (pmat, pmat_d), (g_mix, g_mix_d), (g_mlp, g_mlp_d),
                             (g_fin, g_fin_d), (s_pool, s_pool_d), (g_sub, g_sub_d), (invc, invc_d), (msk, msk_d)):
                S.dma("sp", dm(dst[:], src), writes=[b_const])
            S.op("pool", lambda e: e.memset(ones_bf[:], 1.0), writes=[b_const])
            S.op("dve", cp(pmat_bf[:], pmat[:]), reads=[b_const], writes=[b_const])

            phase()
            xin = apool(3, [1024], F32, "xin")
            rr = 0
            for t in range(NT):
                xi, xib = xin.get()
                S.dma("sp", dm(xi, x_loc[t * 128:(t + 1) * 128, :]), writes=[xib])
                bi = 0 if t == 0 else 1 + (t - 1) // 4
                for half in range(2):
                    ps, psbuf = ps_all[rr % 8]
                    rr += 1
                    for kk in range(4):
                        k = half * 4 + kk
                        S.op("pe", (lambda o, i_: (lambda e: e.transpose(o, i_, ident[:])))(
                            ps[:, kk * 128:(kk + 1) * 128], xi[:, k * 128:(k + 1) * 128]),
                            reads=[xib, b_const], writes=[psbuf], inc=(kk == 3))
                    dst = xT[:, half * 4:half * 4 + 4, t * 128:(t + 1) * 128]
                    srcv = ps[:, :].rearrange("p (a b) -> p a b", b=128)
                    eng = "act" if (rr % 2) else "dve"
                    if eng == "act":
                        S.op("act", (lambda o, i_: (lambda e: e.copy(out=o, in_=i_)))(dst, srcv),
                             reads=[psbuf], writes=[b_x[bi][half * 4 + kk] for kk in range(4)])
                    else:
                        S.op("dve", cp(dst, srcv), reads=[psbuf], writes=[b_x[bi][half * 4 + kk] for kk in range(4)])

            stage(0)
            phase()
            c5, c5b = aalloc([KC, 2], F32, "c5")
            c5s, c5sb = aalloc([KC, 2], BF16, "c5s")
            lamv, lamvb = aalloc([4, DEPTH * 64], F32, "lamv")
            lamp, lampb = aalloc([2, DEPTH * 64], F32, "lamp")
            lams, lamsb = aalloc([2, DEPTH], F32, "lams")
            adap, adapb = aalloc([192], F32, "adap")
            adaall, adaallb = aalloc([2, 192], F32, "adaall")
            badas, badasb = aalloc([DEPTH, 48], F32, "badas")
            wa = apool(8, [3072], BF16, "wa")
            S.dma("sp", dm(c5, c5T_d), writes=[c5b])
            S.dma("sp", dm(lamv, lamv_d.rearrange("p w l d -> p w (l d)")), writes=[lamvb])
            S.dma("sp", dm(badas, b_ada_d), writes=[badasb])
            c5e, c5eb = aalloc([KC, 2], F32, "c5e")
            S.op("act", act(c5e, c5, AF.Exp, scale=-1.0), reads=[c5b], writes=[c5eb])
            S.op("dve", ts(c5e, c5e, 1.0, None, ALU.add), reads=[c5eb], writes=[c5eb])
            S.op("dve", lambda e: e.reciprocal(out=c5e, in_=c5e), reads=[c5eb], writes=[c5eb])
            S.op("dve", tt(c5s, c5, c5e, ALU.mult), reads=[c5b, c5eb], writes=[c5sb])
            ps, psbuf = ps_all[0]
            for l in range(DEPTH):
                wts = []
                for k in range(KC):
                    wt, wtb = wa.get()
                    S.dma("pool", dm(wt, w_ada_d[l, k * 128:(k + 1) * 128, :]), writes=[wtb])
                    wts.append((wt, wtb))
                for m in range(24):
                    col = (l * 24 + m) * 2
                    for k in range(KC):
                        wt, wtb = wts[k]
                        S.op("pe", mm(ps[:, col:col + 2], wt[:, m * 128:(m + 1) * 128], c5s[:, k, :],
                                      k == 0, k == KC - 1),
                             reads=[wtb, c5sb], writes=[psbuf], inc=(k == KC - 1))
            S.op("dve", cp(adap, ps[:, 0:192]), reads=[psbuf], writes=[adapb])
            b_as, b_ar = Buf("as"), Buf("ar")
            S.dma("sp", dm(ada_send.ap(), adap), reads=[adapb], writes=[b_as])
            S.coll(lambda e: e.collective_compute(
                "AllGather", ALU.bypass, replica_groups=[[0, 1], [2, 3], [4, 5], [6, 7]],
                ins=[ada_send.ap().opt()], outs=[ada_recv.ap().opt()]), reads=[b_as], writes=[b_ar])
            S.dma("sp", dm(adaall, ada_recv.ap().rearrange("(r p) n -> p r n", p=128)), reads=[b_ar], writes=[adaallb])
            for r in range(2):
                av = adaall[:, r, :].rearrange("p (l m v) -> p l m v", l=DEPTH, m=24)
                bb = badas[:, :, r * 24:(r + 1) * 24]
                S.op("dve", tt(ada_c[:, :, r * 24:(r + 1) * 24], av[:, :, :, 0], bb, ALU.add),
                     reads=[adaallb, badasb], writes=[b_ada])
                S.op("dve", tt(ada_l[:, :, r * 24:(r + 1) * 24], av[:, :, :, 1], bb, ALU.add),
                     reads=[adaallb, badasb], writes=[b_ada])
            for wi, adat in enumerate((ada_c, ada_l)):
                S.op("dve", stt(gc[:, wi, 0], adat[:, :, 8:16], 1.0, g_mix[:], ALU.add, ALU.mult),
                     reads=[b_ada, b_const], writes=[b_ada])
                S.op("dve", stt(gc[:, wi, 1], adat[:, :, 32:40], 1.0, g_mlp[:], ALU.add, ALU.mult),
                     reads=[b_ada, b_const], writes=[b_ada])
            S.op("dve", tt(lamp[:, 0, :], lamv[:, 0, :], lamv[:, 1, :], ALU.mult), reads=[lamvb], writes=[lampb])
            S.op("dve", tt(lamp[:, 1, :], lamv[:, 2, :], lamv[:, 3, :], ALU.mult), reads=[lamvb], writes=[lampb])
            S.op("dve", lambda e: e.reduce_sum(out=lams[:].rearrange("p a l -> p (a l)"),
                                               in_=lamp[:].rearrange("p a (l d) -> p (a l) d", d=64), axis=AX.X),
                 reads=[lampb], writes=[lamsb])
            S.op("act", act(lams, lams, AF.Exp), reads=[lamsb], writes=[lamsb])
            S.op("dve", tt(nlam[:], lams[:, 1, :], lams[:, 0, :], ALU.subtract), reads=[lamsb], writes=[b_ada])
            lam_init = [0.8 - 0.6 * math.exp(-0.3 * l) for l in range(DEPTH)]
            for l in range(DEPTH):
                S.op("dve", ts(nlam[:, l:l + 1], nlam[:, l:l + 1], -lam_init[l], None, ALU.add),
                     reads=[b_ada], writes=[b_ada])
                S.op("dve", ts(g_sub[:, l:l + 1], g_sub[:, l:l + 1], 1.0 - lam_init[l], None, ALU.mult),
                     reads=[b_const, b_ada], writes=[b_ada])

            stage(1)
            def norm_mod(i, l, kind, pools, psrot):
                c0, N = BLK[i]
                wi = 0 if i == 0 else 1
                adat = ada_c if i == 0 else ada_l
                sh0 = 0 if kind == 0 else 24
                sqp, rsp, xrp = pools
                ps, psbuf = psrot.get()
                for k in range(KC):
                    sq, sqb = sqp.get()
                    eng = "act" if k % 2 == 0 else "pool"
                    if eng == "act":
                        S.op("act", act(sq[:, :N], xT[:, k, c0:c0 + N], AF.Square), reads=[b_x[i][k]], writes=[sqb])
                    else:
                        S.op("pool", tt(sq[:, :N], xT[:, k, c0:c0 + N], xT[:, k, c0:c0 + N], ALU.mult),
                             reads=[b_x[i][k]], writes=[sqb])
                    S.op("pe", mm(ps[:, :N], ones_bf[:], sq[:, :N], k == 0, k == KC - 1),
                         reads=[sqb, b_const], writes=[psbuf], inc=True)
                rs, rsb = rsp.get()
                S.op("act", act(rs[:, :N], ps[:, :N], AF.Ln, bias=EPS, scale=1.0 / D), reads=[psbuf], writes=[rsb])
                S.op("act", act(rs[:, :N], rs[:, :N], AF.Exp, scale=-0.5), reads=[rsb], writes=[rsb])
                for k in range(KC):
                    xr, xrb = xrp.get()
                    e1 = "dve" if k % 2 == 0 else "pool"
                    S.op(e1, tt(xr[:, :N], xT[:, k, c0:c0 + N], rs[:, :N], ALU.mult),
                         reads=[b_x[i][k], rsb], writes=[xrb])
                    S.op("act", act(aT[:, k, c0:c0 + N], xr[:, :N], AF.Identity,
                                    bias=adat[:, l, sh0 + k:sh0 + k + 1], scale=gc[:, wi, kind, l, k:k + 1]),
                         reads=[xrb, b_ada], writes=[b_a[i][k]])

            def load_w(dst, dstb, src_rows_ap):
                S.dma("pool", dm(dst, src_rows_ap), writes=[dstb])

            send_k = send.ap()[0:2176, :].rearrange("a b -> (a b)").rearrange("(h p t) -> h p t", h=4, p=128)
            send_v = send.ap()[2176:4352, :].rearrange("a b -> (a b)").rearrange("(h t d) -> t h d", h=4, d=128)
            send_u = send.ap()[4352:4384, :].rearrange("a b -> (a b)").rearrange("(g p c) -> p g c", g=4, p=128)

            for l in range(n_layers):
                last = (l == DEPTH - 1)
                phase()
                win, winb = aalloc([KC, 2048], BF16, "win")
                winbs = [Buf("win%d" % k).inherit(winb) for k in range(KC)]
                A["bufs"].extend(winbs)
                sqp = apool(3, [512], BF16, "sq")
                rsp = apool(1, [512], F32, "rs")
                xrp = apool(2, [512], F32, "xr")
                ropep = apool(2, [2, 512], F32, "rope")
                tqp = apool(2, [512], BF16, "tq")
                t12p = apool(3, [512], F32, "t12")
                stp = apool(3, [512], BF16, "stg")
                psrot = Rot(ps_all)
                for k in range(KC):
                    load_w(win[:, k, :], winbs[k], w_in_d[l, k * 128:(k + 1) * 128, :])
                for i in range(5):
                    norm_mod(i, l, 0, (sqp, rsp, xrp), psrot)
                for i in range(5):
                    c0, N = BLK[i]
                    if i > 0:
                        rp, rpb = ropep.get()
                        lc = c0 - 128
                        S.dma("sp", dm(rp[:, 0, :], cos_d[:, lc:lc + 512]), writes=[rpb])
                        S.dma("sp", dm(rp[:, 1, :], sin_d[:, lc:lc + 512]), writes=[rpb])
                    for m in range(12):
                        ps, psbuf = psrot.get()
                        for k in range(KC):
                            S.op("pe", mm(ps[:, :N], win[:, k, m * 128:(m + 1) * 128], aT[:, k, c0:c0 + N],
                                          k == 0, k == KC - 1),
                                 reads=[winbs[k], b_a[i][k]], writes=[psbuf], inc=(k == KC - 1))
                        if m < 4:
                            uc = ucol(c0)
                            S.op("act", act(uT[:, m, uc:uc + N], ps[:, :N], AF.Identity), reads=[psbuf], writes=[b_u[m][i]])
                            continue
                        isq = m < 8
                        hp = (m - 4) % 4
                        if isq:
                            dst, dstb = qT[:, hp, c0:c0 + N], b_q[i][hp]
                        else:
                            sg, sgb = stp.get()
                            dst, dstb = sg[:, :N], sgb
                        if i == 0:
                            S.op("act", act(dst, ps[:, :N], AF.Identity), reads=[psbuf], writes=[dstb])
                        else:
                            tq, tqb = tqp.get()
                            S.op("act", act(tq, ps[:, :N], AF.Identity), reads=[psbuf], writes=[tqb])
                            ps2, ps2b = psrot.get()
                            S.op("pe", mm(ps2[:, :N], pmat_bf[:], tq, True, True), reads=[tqb, b_const], writes=[ps2b])
                            t1, t1b = t12p.get()
                            S.op("pool", tt(t1, tq, rp[:, 0, :], ALU.mult), reads=[tqb, rpb], writes=[t1b])
                            t2, t2b = t12p.get()
                            S.op("dve", tt(t2, ps2[:, :N], rp[:, 1, :], ALU.mult), reads=[ps2b, rpb], writes=[t2b])
                            S.op("pool", tt(dst, t1, t2, ALU.add), reads=[t1b, t2b], writes=[dstb])
                        if not isq:
                            S.dma("sp", dm(send_k[hp, :, c0:c0 + N], dst), reads=[dstb], writes=[b_send_k[hp]])
                    for tt_ in range(N // 128):
                        ps, psbuf = psrot.get()
                        tc0 = c0 + tt_ * 128
                        for k in range(KC):
                            S.op("pe", mm(ps[:, :], aT[:, k, tc0:tc0 + 128], win[:, k, 1536:2048], k == 0, k == KC - 1),
                                 reads=[winbs[k], b_a[i][k]], writes=[psbuf], inc=(k == KC - 1))
                        sg, sgb = stp.get()
                        S.op("dve", cp(sg, ps[:, :]), reads=[psbuf], writes=[sgb])
                        S.dma("sp", dm(send_v[tc0:tc0 + 128, :, :], sg.rearrange("p (h d) -> p h d", d=128)),
                              reads=[sgb], writes=b_send_v)
                for reg, tl0 in enumerate((0, 120, 128, 2168)):
                    uc = ucol(tl0)
                    S.dma("sp", dm(send_u[:, :, reg * 8:(reg + 1) * 8], uT[:, :, uc:uc + 8]),
                          reads=[b_u[g][ii] for g in range(4) for ii in range(5)], writes=[b_send_u])
                stage(2)
                def gather(s0, n, rb, wb):
                    S.coll(lambda e: e.collective_compute(
                        "AllGather", ALU.bypass, replica_groups=[[0, 1], [2, 3], [4, 5], [6, 7]],
                        ins=[send.ap()[s0:s0 + n, :].opt()], outs=[recv.ap()[2 * s0:2 * s0 + 2 * n, :].opt()]),
                        reads=[rb], writes=[wb])
                gather(4352, 32, b_send_u, b_ru)
                for h in range(4):
                    gather(h * 544, 544, b_send_k[h], b_rk[h])
                    gather(2176 + h * 544, 544, b_send_v[h], b_rv[h])

                stage(3)
                phase()
                ktp = apool(2, [NKT * 128], BF16, "kt")
                vhp = apool(2, [NKT, 128], BF16, "vh")
                ptp = apool(4, [512], BF16, "pt")
                f32p = apool(6, [512], F32, "f32")
                hal, halb = aalloc([2, 4, 32], BF16, "hal")
                wpl, wplb = aalloc([4, 128], BF16, "wpl")
                pmp = apool(3, [528], F32, "pm")
                sp_, spb_ = aalloc([512], F32, "pS")
                yp = apool(6, [512], BF16, "y")
                etmp, etmpb = aalloc([8], F32, "etmp")
                blocks = [i for i in range(5) if not (last and i == 0)]

                def recv_piece(s0, n, r):
                    return recv.ap()[2 * s0 + r * n: 2 * s0 + (r + 1) * n, :].rearrange("a b -> (a b)")

                def load_kv(h):
                    kt, ktb = ktp.get()
                    vh, vhb = vhp.get()
                    for r in range(2):
                        S.dma("sp", dm(kt[:, r * TL:(r + 1) * TL],
                                       recv_piece(h * 544, 544, r).rearrange("(p t) -> p t", p=128)),
                              reads=[b_rk[h]], writes=[ktb])
                        S.dma("sp", dm(vh[:, r * NT:(r + 1) * NT, :],
                                       recv_piece(2176 + h * 544, 544, r).rearrange("(t p d) -> p t d", p=128, d=128)),
                              reads=[b_rv[h]], writes=[vhb])
                    return kt, ktb, vh, vhb

                kv = load_kv(0)
                for r in range(2):
                    S.dma("sp", dm(hal[:, r], recv_piece(4352, 32, r).rearrange("(g p c) -> p g c", g=4, p=128)),
                          reads=[b_ru], writes=[halb])
                S.dma("pool", dm(wpl, w_pool_d[l].rearrange("g c d -> c g d")), writes=[wplb])
                for (dcol, r, scol, mi) in ((0, 0, 8, 0), (136, 1, 0, 1), (144, 0, 24, 0), (2200, 1, 16, 1)):
                    S.op("dve", ts(uT[:, :, dcol:dcol + 8], hal[:, r, :, scol:scol + 8], msk[:, mi:mi + 1], None, ALU.mult),
                         reads=[halb, b_const], writes=[b_uh])

                ps_misc = Rot([ps_all[7]])

                def pool_ops(i):
                    c0, N = BLK[i]
                    uc = ucol(c0)
                    ys = []
                    for g in range(4):
                        rd = [b_u[g][ii] for ii in range(max(0, i - 1), min(5, i + 2))] + [b_uh]
                        w = 2 << g
                        h = w // 2
                        if g == 0:
                            Sv, Sb = sp_, spb_
                            S.op("pool", tt(Sv[:, :N], uT[:, 0, uc - 1:uc - 1 + N], uT[:, 0, uc:uc + N], ALU.add),
                                 reads=rd, writes=[Sb])
                        else:
                            n2 = N + 2 * h - 2
                            p2, p2b = pmp.get()
                            S.op("pool", tt(p2[:, :n2], uT[:, g, uc - h:uc - h + n2], uT[:, g, uc - h + 1:uc - h + 1 + n2], ALU.add),
                                 reads=rd, writes=[p2b])
                            cur, curb, span, ncur = p2, p2b, 2, n2
                            while span < h:
                                nn = ncur - span
                                nx, nxb = pmp.get()
                                S.op("pool", tt(nx[:, :nn], cur[:, 0:nn], cur[:, span:span + nn], ALU.add),
                                     reads=[curb], writes=[nxb])
                                cur, curb, span, ncur = nx, nxb, span * 2, nn
                            assert ncur == N + h, (ncur, N, h)
                            Sv, Sb = sp_, spb_
                            S.op("pool", tt(Sv[:, :N], cur[:, 0:N], cur[:, h:h + N], ALU.add), reads=[curb], writes=[Sb])
                        y, yb = yp.get()
                        S.op("dve", stt(y[:, :N], Sv[:, :N], 1.0 / w, uT[:, g, uc:uc + N], ALU.mult, ALU.subtract),
                             reads=[Sb] + rd, writes=[yb])
                        regs = {0: ((0, 0), (1, 120)), 1: ((2, 0),), 4: ((3, 504),)}.get(i, ())
                        for reg, col in regs:
                            S.op("pool", tt(etmp, Sv[:, col:col + 8], invc[:, g, reg, :], ALU.mult),
                                 reads=[Sb, b_const], writes=[etmpb])
                            S.op("pool", tt(y[:, col:col + 8], etmp, uT[:, g, uc + col:uc + col + 8], ALU.subtract),
                                 reads=[etmpb] + rd, writes=[yb])
                        ys.append((y, yb))
                    return ys

                def pool_mm(i, ys):
                    c0, N = BLK[i]
                    for g in range(4):
                        y, yb = ys[g]
                        ps, psbuf = ps_misc.get()
                        S.op("pe", mm(ps[:, :N], wpl[:, g, :], y[:, :N], True, True), reads=[wplb, yb], writes=[psbuf])
                        S.op("act", act(aT[:, g, c0:c0 + N], ps[:, :N], AF.Identity, scale=s_pool[:, l, g:g + 1]),
                             reads=[psbuf, b_const], writes=[b_a[i][g]])

                ys_next = pool_ops(blocks[0])

                stage(4)
                ps_s = Rot([ps_all[0], ps_all[1], ps_all[2]])
                ps_acc = [[ps_all[3], ps_all[4]], [ps_all[5], ps_all[6]]]
                for h in range(4):
                    kt, ktb, vh, vhb = kv
                    if h + 1 < 4:
                        kv = load_kv(h + 1)
                    for bi_, i in enumerate(blocks):
                        c0, N = BLK[i]
                        kts = [0, NT] if i == 0 else list(range(NKT))
                        for j in range(2):
                            (po, pob), (pl, plb) = ps_acc[j]
                            pend = None
                            for n_, ktile in enumerate(kts):
                                pss, pssb = ps_s.get()
                                S.op("pe", mm(pss[:, :N], kt[64 * j:64 * j + 64, ktile * 128:(ktile + 1) * 128],
                                              qT[64 * j:64 * j + 64, h, c0:c0 + N], True, True),
                                     reads=[ktb, b_q[i][h]], writes=[pssb])
                                pt, ptb = ptp.get()
                                S.op("act", act(pt[:, :N], pss[:, :N], AF.Exp, scale=0.125), reads=[pssb], writes=[ptb])
                                if pend is not None:
                                    pend()

                                def pv(pt=pt, ptb=ptb, ktile=ktile, first=(n_ == 0), lastk=(n_ == len(kts) - 1),
                                       po=po, pob=pob, pl=pl, plb=plb, N=N):
                                    S.op("pe", mm(po[:, :N], vh[:, ktile, :], pt[:, :N], first, lastk),
                                         reads=[vhb, ptb], writes=[pob], inc=False)
                                    S.op("pe", mm(pl[:, :N], ones_bf[:], pt[:, :N], first, lastk),
                                         reads=[ptb, b_const], writes=[plb], inc=True)
                                pend = pv
                            pend()
                        (po0, pob0), (pl0, plb0) = ps_acc[0]
                        (po1, pob1), (pl1, plb1) = ps_acc[1]
                        r1, r1b = f32p.get()
                        r2, r2b = f32p.get()
                        o1, o1b = f32p.get()
                        o2, o2b = f32p.get()
                        S.op("dve", lambda e, o=r1[:, :N], i_=pl0[:, :N]: e.reciprocal(out=o, in_=i_), reads=[plb0], writes=[r1b])
                        S.op("dve", tt(o1[:, :N], po0[:, :N], r1[:, :N], ALU.mult), reads=[pob0, r1b], writes=[o1b])
                        S.op("dve", lambda e, o=r2[:, :N], i_=pl1[:, :N]: e.reciprocal(out=o, in_=i_), reads=[plb1], writes=[r2b])
                        S.op("dve", tt(o2[:, :N], po1[:, :N], r2[:, :N], ALU.mult), reads=[pob1, r2b], writes=[o2b])
                        S.op("dve", stt(o1[:, :N], o2[:, :N], nlam[:, l:l + 1], o1[:, :N], ALU.mult, ALU.add),
                             reads=[o2b, o1b, b_ada], writes=[o1b])
                        sq, sqb = ptp.get()
                        S.op("dve", tt(sq[:, :N], o1[:, :N], o1[:, :N], ALU.mult), reads=[o1b], writes=[sqb])
                        ps, psbuf = ps_misc.get()
                        S.op("pe", mm(ps[:, :N], ones_bf[:], sq[:, :N], True, True), reads=[sqb, b_const], writes=[psbuf])
                        S.op("act", act(r1[:, :N], ps[:, :N], AF.Ln, bias=EPS, scale=1.0 / 128), reads=[psbuf], writes=[r1b])
                        S.op("act", act(r1[:, :N], r1[:, :N], AF.Exp, scale=-0.5), reads=[r1b], writes=[r1b])
                        S.op("dve", stt(aT[:, 4 + h, c0:c0 + N], o1[:, :N], g_sub[:, l:l + 1], r1[:, :N], ALU.mult, ALU.mult),
                             reads=[o1b, r1b, b_ada], writes=[b_a[i][4 + h]])
                        if h == 0:
                            pool_mm(i, ys_next)
                            if bi_ + 1 < len(blocks):
                                ys_next = pool_ops(blocks[bi_ + 1])

                stage(5)
                phase()
                wo, wob = aalloc([KC, D], BF16, "wo")
                wobs = [Buf("wo%d" % k).inherit(wob) for k in range(KC)]
                A["bufs"].extend(wobs)
                for k in range(KC):
                    load_w(wo[:, k, :], wobs[k], w_out_d[l, k * 128:(k + 1) * 128, :])
                sqp = apool(2, [512], BF16, "sq")
                rsp = apool(1, [512], F32, "rs")
                xrp = apool(2, [512], F32, "xr")
                w1p = apool(2, [KC, 512], BF16, "w1")
                w2p = apool(2, [4, D], BF16, "w2")
                hidp = apool(2, [4, 512], BF16, "hid")
                rlp = apool(2, [512], BF16, "rl")
                psrot = Rot(ps_all)
                adat_of = lambda i: ada_c if i == 0 else ada_l

                def load_mlp(J):
                    w1, w1b = w1p.get()
                    w2, w2b = w2p.get()
                    for k in range(KC):
                        S.dma("pool", dm(w1[:, k, :], w1_d[l, k * 128:(k + 1) * 128, J * 512:(J + 1) * 512]), writes=[w1b])
                    for c in range(4):
                        S.dma("pool", dm(w2[:, c, :], w2_d[l, J * 512 + c * 128: J * 512 + (c + 1) * 128, :]), writes=[w2b])
                    return w1, w1b, w2, w2b

                for i in blocks:
                    c0, N = BLK[i]
                    for m in range(KC):
                        ps, psbuf = psrot.get()
                        for k in range(KC):
                            S.op("pe", mm(ps[:, :N], wo[:, k, m * 128:(m + 1) * 128], aT[:, k, c0:c0 + N], k == 0, k == KC - 1),
                                 reads=[wobs[k], b_a[i][k]], writes=[psbuf], inc=(k == KC - 1))
                        S.op("dve", stt(xT[:, m, c0:c0 + N], ps[:, :N], adat_of(i)[:, l, 16 + m:17 + m], xT[:, m, c0:c0 + N],
                                        ALU.mult, ALU.add),
                             reads=[psbuf, b_ada, b_x[i][m]], writes=[b_x[i][m]])
                mw = load_mlp(0)
                stage(6)
                for i in blocks:
                    norm_mod(i, l, 1, (sqp, rsp, xrp), psrot)
                stage(8)
                steps = [(J, i) for J in range(8) for i in blocks]
                wts = {0: mw}

                def mlp1(J, i):
                    c0, N = BLK[i]
                    w1, w1b, w2, w2b = wts[J]
                    hid, hidb = hidp.get()
                    for c in range(4):
                        ps, psbuf = psrot.get()
                        for k in range(KC):
                            S.op("pe", mm(ps[:, :N], w1[:, k, c * 128:(c + 1) * 128], aT[:, k, c0:c0 + N], k == 0, k == KC - 1),
                                 reads=[w1b, b_a[i][k]], writes=[psbuf], inc=(k == KC - 1))
                        rl, rlb = rlp.get()
                        S.op("act", act(rl[:, :N], ps[:, :N], AF.Relu), reads=[psbuf], writes=[rlb])
                        S.op("dve", tt(hid[:, c, :N], rl[:, :N], rl[:, :N], ALU.mult), reads=[rlb], writes=[hidb])
                    return hid, hidb

                def mlp2(J, i, hid, hidb):
                    c0, N = BLK[i]
                    w1, w1b, w2, w2b = wts[J]
                    for m in range(KC):
                        ps, psbuf = psrot.get()
                        for c in range(4):
                            S.op("pe", mm(ps[:, :N], w2[:, c, m * 128:(m + 1) * 128], hid[:, c, :N], c == 0, c == 3),
                                 reads=[w2b, hidb], writes=[psbuf], inc=(c == 3))
                        S.op("dve", stt(xT[:, m, c0:c0 + N], ps[:, :N], adat_of(i)[:, l, 40 + m:41 + m], xT[:, m, c0:c0 + N],
                                        ALU.mult, ALU.add),
                             reads=[psbuf, b_ada, b_x[i][m]], writes=[b_x[i][m]])

                cur_h = None
                for n_, (J, i) in enumerate(steps):
                    if i == blocks[0] and J + 1 < 8:
                        wts[J + 1] = load_mlp(J + 1)
                    if n_ == 0:
                        cur_h = mlp1(J, i)
                    nxt_h = None
                    if n_ + 1 < len(steps):
                        nxt_h = mlp1(*steps[n_ + 1])
                    mlp2(J, i, *cur_h)
                    cur_h = nxt_h

            stage(7)
            phase()
            sqp = apool(3, [512], BF16, "sq")
            rsp = apool(1, [512], F32, "rs")
            ynp = apool(3, [512], F32, "yn")
            otp = apool(2, [D], F32, "ot")
            psrot = Rot(ps_all)
            for i in range(1, 5):
                c0, N = BLK[i]
                ps, psbuf = psrot.get()
                for k in range(KC):
                    sq, sqb = sqp.get()
                    S.op("act", act(sq[:, :N], xT[:, k, c0:c0 + N], AF.Square), reads=[b_x[i][k]], writes=[sqb])
                    S.op("pe", mm(ps[:, :N], ones_bf[:], sq[:, :N], k == 0, k == KC - 1),
                         reads=[sqb, b_const], writes=[psbuf], inc=True)
                rs, rsb = rsp.get()
                S.op("act", act(rs[:, :N], ps[:, :N], AF.Ln, bias=EPS, scale=1.0 / D), reads=[psbuf], writes=[rsb])
                S.op("act", act(rs[:, :N], rs[:, :N], AF.Exp, scale=-0.5), reads=[rsb], writes=[rsb])
                yns = []
                for k in range(KC):
                    pass
                for tt_ in range(4):
                    ot, otb = otp.get()
                    for half in range(2):
                        ps2, ps2b = psrot.get()
                        for kk in range(4):
                            k = half * 4 + kk
                            yn, ynb = ynp.get()
                            tc0 = c0 + tt_ * 128
                            S.op("dve", stt(yn[:, :128], xT[:, k, tc0:tc0 + 128], g_fin[:, k:k + 1],
                                            rs[:, tt_ * 128:(tt_ + 1) * 128], ALU.mult, ALU.mult),
                                 reads=[b_x[i][k], rsb, b_const], writes=[ynb])
                            S.op("pe", (lambda o, i_: (lambda e: e.transpose(o, i_, ident[:])))(
                                ps2[:, kk * 128:(kk + 1) * 128], yn[:, :128]),
                                reads=[ynb, b_const], writes=[ps2b], inc=True)
                        S.op("act", (lambda o, i_: (lambda e: e.copy(out=o, in_=i_)))(ot[:, half * 512:(half + 1) * 512], ps2[:, :]),
                             reads=[ps2b], writes=[otb])
                    r0 = (i - 1) * 512 + tt_ * 128
                    S.dma("sp", dm(out_d[r0:r0 + 128, :], ot), reads=[otb], writes=[b_out])
        except _Stop:
            pass
        b_out.inherit(b_ru, b_send_u, *(b_rk + b_rv + b_send_k + b_send_v))
        if dump == "mix":
            S.dma("pool", dm(dump_d, aT[:]), reads=[b_a[i][k] for i in range(5) for k in range(KC)], writes=[b_out])
        elif dump is not None:
            S.dma("sp", dm(dump_d, xT[:]), reads=[b_x[i][k] for i in range(5) for k in range(KC)], writes=[b_out])
        S.wait_all("sp", [b_out])
        S.run(st)
    return nc


def _rope_tables(r):
    n = np.arange(2048 * r, 2048 * (r + 1))
    row = (n // 64).astype(np.float32)
    col = (n % 64).astype(np.float32)
    inv = (np.float32(10000.0) ** (-np.arange(0, 32, 2, dtype=np.float32) / np.float32(32))).astype(np.float32)
    ang = np.concatenate([row[:, None] * inv, col[:, None] * inv], axis=-1).astype(np.float32)
    cos = np.cos(ang).astype(np.float32).T
    sin = np.sin(ang).astype(np.float32).T
    return np.ascontiguousarray(np.tile(cos, (4, 1))), np.ascontiguousarray(np.tile(sin, (4, 1)))


def _pool_tables(r):
    invc = np.zeros((4, 4, 8), np.float32)
    for g in range(4):
        w = 2 << g
        for reg in range(4):
            n = 256 if reg < 2 else 4096
            half = n // 2
            for i in range(8):
                tl = i if reg % 2 == 0 else half - 8 + i
                t = r * half + tl
                lo = max(t - w // 2, 0)
                hi = min(t + w - w // 2, n)
                invc[g, reg, i] = 1.0 / float(hi - lo)
    msk = np.array([1.0, 0.0] if r == 1 else [0.0, 1.0], np.float32)
    return (np.ascontiguousarray(np.broadcast_to(invc, (128, 4, 4, 8))),
            np.ascontiguousarray(np.broadcast_to(msk, (128, 2))))


def make_in_maps(x, c, ctx, c_ctx, w_ada, b_ada, g_mix, g_mlp, w_in, w_pool, s_pool,
                 lam_q1, lam_k1, lam_q2, lam_k2, g_subln, w_out, w_mlp1, w_mlp2, g_final, n_layers=DEPTH):
    f = lambda a: np.ascontiguousarray(np.asarray(a, dtype=np.float32))
    x, c, ctx, c_ctx = f(x), f(c), f(ctx), f(c_ctx)
    w_ada = f(w_ada)
    w_in, w_pool, w_out, w_mlp1, w_mlp2 = [f(np.asarray(a)[:n_layers]) for a in (w_in, w_pool, w_out, w_mlp1, w_mlp2)]
    fm = lambda a, nch: f(np.asarray(a, np.float32).reshape(a.shape[0], nch, 128).transpose(2, 0, 1))
    b_adaT = fm(b_ada, 48)
    g_mixT = fm(g_mix, KC)
    g_mlpT = fm(g_mlp, KC)
    g_finT = f(np.asarray(g_final, np.float32).reshape(KC, 128).T)
    s_poolT = fm(s_pool, 4)
    g_subT = f(np.asarray(g_subln, np.float32).T)
    lam = np.stack([np.asarray(a, np.float32) for a in (lam_q1, lam_k1, lam_q2, lam_k2)], axis=0)
    lamv = f(np.broadcast_to(lam[None], (128, 4, DEPTH, 64)))
    ident = np.eye(128, dtype=np.float32)
    pmat = np.zeros((128, 128), np.float32)
    for m in range(128):
        if m % 64 < 32:
            pmat[m + 32, m] = -1.0
        else:
            pmat[m - 32, m] = 1.0
    maps = []
    for core in range(8):
        b, r = core // 2, core % 2
        x_loc = np.concatenate([ctx[b, 128 * r:128 * (r + 1)], x[b, 2048 * r:2048 * (r + 1)]], axis=0)
        c2 = np.stack([c_ctx, c[b]], axis=0)
        c2T = f(c2.reshape(2, KC, 128).transpose(2, 1, 0))
        cosT, sinT = _rope_tables(r)
        invc, msk = _pool_tables(r)
        maps.append({
            "x_loc": f(x_loc), "c2T": c2T,
            "w_ada_sl": f(w_ada[:, :, r * 3072:(r + 1) * 3072]),
            "b_adaT": b_adaT, "g_mixT": g_mixT, "g_mlpT": g_mlpT, "g_finT": g_finT,
            "s_poolT": s_poolT, "g_subT": g_subT, "lamv": lamv,
            "w_in": w_in, "w_pool": w_pool, "w_out": w_out, "w_mlp1": w_mlp1, "w_mlp2": w_mlp2,
            "cosT": cosT, "sinT": sinT, "invc": invc, "msk": msk, "ident": ident, "pmat": pmat,
        })
    return maps


_NC_CACHE = {}


def kernel(**inputs):
    maps = make_in_maps(**inputs)
    if "nc" not in _NC_CACHE:
        _NC_CACHE["nc"] = build_program()
    res = run_bass_kernel_spmd(_NC_CACHE["nc"], maps, core_ids=list(range(8)))
    out = np.empty((4, 4096, D), np.float32)
    for core in range(8):
        b, r = core // 2, core % 2
        out[b, 2048 * r:2048 * (r + 1)] = res.results[core]["out"]
    return out
```

```python
import contextlib
import math
import numpy as np
import concourse.bass as bass
import concourse.mybir as mybir
from concourse.bass_utils import run_bass_kernel_spmd

F32 = mybir.dt.float32
BF16 = mybir.dt.bfloat16
AF = mybir.ActivationFunctionType
ALU = mybir.AluOpType
AX = mybir.AxisListType

DEPTH = 4
D = 1024
KC = 8
TL = 2176
NT = 17
NKT = 34
EPS = 1e-6
SEND_ROWS = 4384
BLK = [(0, 128)] + [(128 + 512 * i, 512) for i in range(4)]
UCOLS = 2208
ENGS = ("pe", "act", "dve", "pool", "sp")
EPOCH = 8000


def ucol(tl):
    return tl + 8 if tl < 128 else tl + 24


class Buf:
    __slots__ = ("name", "w", "r")

    def __init__(self, name=""):
        self.name = name
        self.w = []
        self.r = []

    def inherit(self, *others):
        for o in others:
            self.r.extend(o.w)
            self.r.extend(o.r)
        return self


class Sched:
    def __init__(self, nc, n_dma_sems=32):
        self.nc = nc
        self.lists = {e: [] for e in ENGS}
        self.cnt = {e: 0 for e in ENGS}
        self.known = {e: {} for e in ENGS}
        self.n_dma_sems = n_dma_sems
        self.dma_val = [0] * n_dma_sems
        self.dma_rr = 0
        self.n_cc_sems = 12
        self.cc_val = [0] * 12
        self.cc_rr = 0
        self.sems = {}

    @staticmethod
    def _ekey(eng, n):
        ep = (n - 1) // EPOCH
        return ((eng, ep), n - ep * EPOCH)

    def _need(self, eng, ev, waits):
        if ev is None:
            return
        k, v = ev
        if self.known[eng].get(k, 0) >= v:
            return
        if eng == "pe" and k[0] == "pe":
            return
        waits[k] = max(waits.get(k, 0), v)

    def _collect(self, eng, reads, writes, acc=False):
        waits = {}
        for b in reads:
            for ev in b.w:
                self._need(eng, ev, waits)
        for b in writes:
            if not acc:
                for ev in b.w:
                    self._need(eng, ev, waits)
            for ev in b.r:
                self._need(eng, ev, waits)
        for k, v in waits.items():
            self.known[eng][k] = v
        return list(waits.items())

    def _record(self, ev, reads, writes, acc=False):
        for b in reads:
            b.r.append(ev)
        for b in writes:
            if acc and not b.r:
                b.w.append(ev)
            else:
                b.w = [ev]
                b.r = []

    def op(self, eng, fn, reads=(), writes=(), inc=True):
        waits = self._collect(eng, reads, writes)
        if inc:
            self.cnt[eng] += 1
            ev = self._ekey(eng, self.cnt[eng])
        else:
            ev = self._ekey(eng, self.cnt[eng] + 1)
        self.lists[eng].append(("op", waits, fn, ev if inc else None))
        self._record(ev, reads, writes)
        return ev

    def dma(self, q, fn, reads=(), writes=(), acc=True):
        waits = self._collect(q, reads, writes, acc)
        i = self.dma_rr
        self.dma_rr = (self.dma_rr + 1) % self.n_dma_sems
        key = ("dma", i)
        prev = self.dma_val[i]
        if prev and self.known[q].get(key, 0) < prev:
            waits.append((key, prev))
            self.known[q][key] = prev
        self.dma_val[i] = prev + 16
        ev = (key, prev + 16)
        self.lists[q].append(("dma", waits, fn, key))
        self._record(ev, reads, writes, acc)
        return ev

    def coll(self, fn, reads=(), writes=()):
        q = "pool"
        waits = self._collect(q, reads, writes)
        i = self.cc_rr
        self.cc_rr = (self.cc_rr + 1) % self.n_cc_sems
        key = ("cc", i)
        prev = self.cc_val[i]
        if prev and self.known[q].get(key, 0) < prev:
            waits.append((key, prev))
            self.known[q][key] = prev
        self.cc_val[i] = prev + 1
        ev = (key, prev + 1)
        self.lists[q].append(("cc", waits, fn, key))
        self._record(ev, reads, writes)
        return ev

    def wait_all(self, eng, bufs):
        waits = self._collect(eng, [], bufs)
        self.lists[eng].append(("wait", waits, None, None))

    def _emit(self, ename, eng):
        sems = self.sems
        for kind, waits, fn, x in self.lists[ename]:
            for k, v in waits:
                eng.wait_ge(sems[k], v)
            if kind == "op":
                ins = fn(eng)
                if x is not None:
                    ins.then_inc(sems[x[0]], 1)
            elif kind == "dma":
                fn(eng).then_inc(sems[x], 16)
            elif kind == "cc":
                fn(eng).then_inc(sems[x], 1)

    def run(self, stack):
        nc = self.nc
        for e in ENGS:
            nep = (max(self.cnt[e], 1) - 1) // EPOCH + 1
            for ep in range(nep + 1):
                self.sems[(e, ep)] = stack.enter_context(nc.semaphore("s_%s%d" % (e, ep)))
        for i in range(self.n_dma_sems):
            self.sems[("dma", i)] = stack.enter_context(nc.semaphore("s_dma%d" % i))
        for i in range(self.n_cc_sems):
            self.sems[("cc", i)] = stack.enter_context(nc.semaphore("s_cc%d" % i))
        block = stack.enter_context(nc.Block())
        block.tensor(lambda eng: self._emit("pe", eng))
        block.scalar(lambda eng: self._emit("act", eng))
        block.vector(lambda eng: self._emit("dve", eng))
        block.gpsimd(lambda eng: self._emit("pool", eng))
        block.sync(lambda eng: self._emit("sp", eng))


class Rot:
    def __init__(self, items):
        self.items = items
        self.i = 0

    def get(self):
        it = self.items[self.i % len(self.items)]
        self.i += 1
        return it


class _Stop(Exception):
    pass


def build_program(n_layers=DEPTH, dump=None, stop=None):
    def stage(k):
        if stop is not None and k == stop:
            raise _Stop()

    nc = bass.Bass("TRN2", target_bir_lowering=False)

    def din(name, shape, dt=F32):
        return nc.dram_tensor(name, list(shape), dt, kind="ExternalInput").ap()

    x_loc = din("x_loc", [TL, D])
    c5T_d = din("c2T", [128, KC, 2])
    w_ada_d = din("w_ada_sl", [DEPTH, D, 3072])
    b_ada_d = din("b_adaT", [128, DEPTH, 48])
    g_mix_d = din("g_mixT", [128, DEPTH, KC])
    g_mlp_d = din("g_mlpT", [128, DEPTH, KC])
    g_fin_d = din("g_finT", [128, KC])
    s_pool_d = din("s_poolT", [128, DEPTH, 4])
    g_sub_d = din("g_subT", [128, DEPTH])
    lamv_d = din("lamv", [128, 4, DEPTH, 64])
    w_in_d = din("w_in", [n_layers, D, 2048])
    w_pool_d = din("w_pool", [n_layers, 4, 128, 128])
    w_out_d = din("w_out", [n_layers, D, D])
    w1_d = din("w_mlp1", [n_layers, D, 4096])
    w2_d = din("w_mlp2", [n_layers, 4096, D])
    cos_d = din("cosT", [128, 2048])
    sin_d = din("sinT", [128, 2048])
    invc_d = din("invc", [128, 4, 4, 8])
    msk_d = din("msk", [128, 2])
    ident_d = din("ident", [128, 128])
    pmat_d = din("pmat", [128, 128])
    out_d = nc.dram_tensor("out", [2048, D], F32, kind="ExternalOutput").ap()
    dump_d = None
    if dump is not None:
        dump_d = nc.dram_tensor("dump", [128, KC, TL], F32, kind="ExternalOutput").ap()

    send = nc.dram_tensor("send", [SEND_ROWS, 512], BF16)
    recv = nc.dram_tensor("recv", [2 * SEND_ROWS, 512], BF16)
    ada_send = nc.dram_tensor("ada_send", [128, 192], F32)
    ada_recv = nc.dram_tensor("ada_recv", [256, 192], F32)

    S = Sched(nc)
    with contextlib.ExitStack() as st:
        def T(name, shape, dt):
            return st.enter_context(nc.sbuf_tensor("sb_" + name, list(shape), dt))

        xT = T("xT", [128, KC, TL], F32)
        aT = T("aT", [128, KC, TL], BF16)
        qT = T("qT", [128, 4, TL], BF16)
        uT = T("uT", [128, 4, UCOLS], BF16)
        ident = T("ident", [128, 128], F32)
        pmat = T("pmat", [128, 128], F32)
        ones_bf = T("ones_bf", [128, 128], BF16)
        pmat_bf = T("pmat_bf", [128, 128], BF16)
        ada_c = T("ada_c", [128, DEPTH, 48], F32)
        ada_l = T("ada_l", [128, DEPTH, 48], F32)
        gc = T("gc", [128, 2, 2, DEPTH, KC], F32)
        g_mix = T("g_mix", [128, DEPTH, KC], F32)
        g_mlp = T("g_mlp", [128, DEPTH, KC], F32)
        g_fin = T("g_fin", [128, KC], F32)
        s_pool = T("s_pool", [128, DEPTH, 4], F32)
        g_sub = T("g_sub", [128, DEPTH], F32)
        nlam = T("nlam", [128, DEPTH], F32)
        invc = T("invc", [128, 4, 4, 8], F32)
        msk = T("msk", [128, 2], F32)
        ARENA = 67584
        arena = T("arena", [128, ARENA // 2], BF16)

        psb = [st.enter_context(nc.psum_tensor("ps%d" % i, [128, 512], F32)) for i in range(8)]
        ps_all = [(psb[i], Buf("ps%d" % i)) for i in range(8)]

        A = {"off": 0, "bufs": [], "old": Buf("old")}

        def phase():
            old = Buf("old")
            old.inherit(A["old"], *A["bufs"])
            A["old"] = old
            A["bufs"] = []
            A["off"] = 0

        def aalloc(shape, dt, name=""):
            n = int(np.prod(shape))
            nb = n * (2 if dt == BF16 else 4)
            nb_al = (nb + 63) // 64 * 64
            off = A["off"]
            assert off + nb_al <= ARENA, (name, off, nb_al)
            A["off"] = off + nb_al
            v = arena[:, off // 2: off // 2 + nb // 2]
            if dt == F32:
                v = v.bitcast(F32)
            if len(shape) == 2:
                v = v.rearrange("p (a b) -> p a b", b=shape[1])
            elif len(shape) == 3:
                v = v.rearrange("p (a b c) -> p a b c", b=shape[1], c=shape[2])
            b = Buf(name).inherit(A["old"])
            A["bufs"].append(b)
            return v, b

        def apool(n, shape, dt, name=""):
            return Rot([aalloc(shape, dt, "%s%d" % (name, i)) for i in range(n)])

        def mm(out, lhsT, rhs, start, stop):
            return lambda e: e.matmul(out, lhsT=lhsT, rhs=rhs, start=start, stop=stop)

        def act(out, in_, func, bias=None, scale=None):
            kw = {}
            if bias is not None:
                kw["bias"] = bias
            if scale is not None:
                kw["scale"] = scale
            return lambda e: e.activation(out=out, in_=in_, func=func, **kw)

        def tt(out, in0, in1, op):
            return lambda e: e.tensor_tensor(out=out, in0=in0, in1=in1, op=op)

        def stt(out, in0, scalar, in1, op0, op1):
            return lambda e: e.scalar_tensor_tensor(out=out, in0=in0, scalar=scalar, in1=in1, op0=op0, op1=op1)

        def ts(out, in0, s1, s2, op0, op1=None):
            if op1 is None:
                return lambda e: e.tensor_scalar(out=out, in0=in0, scalar1=s1, scalar2=None, op0=op0)
            return lambda e: e.tensor_scalar(out=out, in0=in0, scalar1=s1, scalar2=s2, op0=op0, op1=op1)

        def cp(out, in_):
            return lambda e: e.tensor_copy(out=out, in_=in_)

        def dm(out, in_):
            return lambda e: e.dma_start(out=out, in_=in_)

        b_x = [[Buf("x%d_%d" % (i, k)) for k in range(KC)] for i in range(5)]
        b_a = [[Buf("a%d_%d" % (i, k)) for k in range(KC)] for i in range(5)]
        b_q = [[Buf("q%d_%d" % (i, h)) for h in range(4)] for i in range(5)]
        b_u = [[Buf("u%d_%d" % (g, i)) for i in range(5)] for g in range(4)]
        b_uh = Buf("uhalo")
        b_const = Buf("const")
        b_ada = Buf("ada")
        b_send_k = [Buf("send_k%d" % h) for h in range(4)]
        b_send_v = [Buf("send_v%d" % h) for h in range(4)]
        b_send_u = Buf("send_u")
        b_rk = [Buf("rk%d" % h) for h in range(4)]
        b_rv = [Buf("rv%d" % h) for h in range(4)]
        b_ru = Buf("ru")

        b_out = Buf("out")
        try:
            for dst, src in ((ident, ident_d), (pmat, pmat_d), (g_mix, g_mix_d), (g_mlp, g_mlp_d),
                             (g_fin, g_fin_d), (s_pool, s_pool_d), (g_sub, g_sub_d), (invc, invc_d), (msk, msk_d)):
                S.dma("sp", dm(dst[:], src), writes=[b_const])
            S.op("pool", lambda e: e.memset(ones_bf[:], 1.0), writes=[b_const])
            S.op("dve", cp(pmat_bf[:], pmat[:]), reads=[b_const], writes=[b_const])

            phase()
            xin = apool(3, [1024], F32, "xin")
            rr = 0
            for t in range(NT):
                xi, xib = xin.get()
                S.dma("sp", dm(xi, x_loc[t * 128:(t + 1) * 128, :]), writes=[xib])
                bi = 0 if t == 0 else 1 + (t - 1) // 4
                for half in range(2):
                    ps, psbuf = ps_all[rr % 8]
                    rr += 1
                    for kk in range(4):
                        k = half * 4 + kk
                        S.op("pe", (lambda o, i_: (lambda e: e.transpose(o, i_, ident[:])))(
                            ps[:, kk * 128:(kk + 1) * 128], xi[:, k * 128:(k + 1) * 128]),
                            reads=[xib, b_const], writes=[psbuf], inc=(kk == 3))
                    dst = xT[:, half * 4:half * 4 + 4, t * 128:(t + 1) * 128]
                    srcv = ps[:, :].rearrange("p (a b) -> p a b", b=128)
                    eng = "act" if (rr % 2) else "dve"
                    if eng == "act":
                        S.op("act", (lambda o, i_: (lambda e: e.copy(out=o, in_=i_)))(dst, srcv),
                             reads=[psbuf], writes=[b_x[bi][half * 4 + kk] for kk in range(4)])
                    else:
                        S.op("dve", cp(dst, srcv), reads=[psbuf], writes=[b_x[bi][half * 4 + kk] for kk in range(4)])

            stage(0)
            phase()
            c5, c5b = aalloc([KC, 2], F32, "c5")
            c5s, c5sb = aalloc([KC, 2], BF16, "c5s")
            lamv, lamvb = aalloc([4, DEPTH * 64], F32, "lamv")
            lamp, lampb = aalloc([2, DEPTH * 64], F32, "lamp")
            lams, lamsb = aalloc([2, DEPTH], F32, "lams")
            adap, adapb = aalloc([192], F32, "adap")
            adaall, adaallb = aalloc([2, 192], F32, "adaall")
            badas, badasb = aalloc([DEPTH, 48], F32, "badas")
            wa = apool(8, [3072], BF16, "wa")
            S.dma("sp", dm(c5, c5T_d), writes=[c5b])
            S.dma("sp", dm(lamv, lamv_d.rearrange("p w l d -> p w (l d)")), writes=[lamvb])
            S.dma("sp", dm(badas, b_ada_d), writes=[badasb])
            c5e, c5eb = aalloc([KC, 2], F32, "c5e")
            S.op("act", act(c5e, c5, AF.Exp, scale=-1.0), reads=[c5b], writes=[c5eb])
            S.op("dve", ts(c5e, c5e, 1.0, None, ALU.add), reads=[c5eb], writes=[c5eb])
            S.op("dve", lambda e: e.reciprocal(out=c5e, in_=c5e), reads=[c5eb], writes=[c5eb])
            S.op("dve", tt(c5s, c5, c5e, ALU.mult), reads=[c5b, c5eb], writes=[c5sb])
            ps, psbuf = ps_all[0]
            for l in range(DEPTH):
                wts = []
                for k in range(KC):
                    wt, wtb = wa.get()
                    S.dma("pool", dm(wt, w_ada_d[l, k * 128:(k + 1) * 128, :]), writes=[wtb])
                    wts.append((wt, wtb))
                for m in range(24):
                    col = (l * 24 + m) * 2
                    for k in range(KC):
                        wt, wtb = wts[k]
                        S.op("pe", mm(ps[:, col:col + 2], wt[:, m * 128:(m + 1) * 128], c5s[:, k, :],
                                      k == 0, k == KC - 1),
                             reads=[wtb, c5sb], writes=[psbuf], inc=(k == KC - 1))
            S.op("dve", cp(adap, ps[:, 0:192]), reads=[psbuf], writes=[adapb])
            b_as, b_ar = Buf("as"), Buf("ar")
            S.dma("sp", dm(ada_send.ap(), adap), reads=[adapb], writes=[b_as])
            S.coll(lambda e: e.collective_compute(
                "AllGather", ALU.bypass, replica_groups=[[0, 1], [2, 3], [4, 5], [6, 7]],
                ins=[ada_send.ap().opt()], outs=[ada_recv.ap().opt()]), reads=[b_as], writes=[b_ar])
            S.dma("sp", dm(adaall, ada_recv.ap().rearrange("(r p) n -> p r n", p=128)), reads=[b_ar], writes=[adaallb])
            for r in range(2):
                av = adaall[:, r, :].rearrange("p (l m v) -> p l m v", l=DEPTH, m=24)
                bb = badas[:, :, r * 24:(r + 1) * 24]
                S.op("dve", tt(ada_c[:, :, r * 24:(r + 1) * 24], av[:, :, :, 0], bb, ALU.add),
                     reads=[adaallb, badasb], writes=[b_ada])
                S.op("dve", tt(ada_l[:, :, r * 24:(r + 1) * 24], av[:, :, :, 1], bb, ALU.add),
                     reads=[adaallb, badasb], writes=[b_ada])
            for wi, adat in enumerate((ada_c, ada_l)):
                S.op("dve", stt(gc[:, wi, 0], adat[:, :, 8:16], 1.0, g_mix[:], ALU.add, ALU.mult),
                     reads=[b_ada, b_const], writes=[b_ada])
                S.op("dve", stt(gc[:, wi, 1], adat[:, :, 32:40], 1.0, g_mlp[:], ALU.add, ALU.mult),
                     reads=[b_ada, b_const], writes=[b_ada])
            S.op("dve", tt(lamp[:, 0, :], lamv[:, 0, :], lamv[:, 1, :], ALU.mult), reads=[lamvb], writes=[lampb])
            S.op("dve", tt(lamp[:, 1, :], lamv[:, 2, :], lamv[:, 3, :], ALU.mult), reads=[lamvb], writes=[lampb])
            S.op("dve", lambda e: e.reduce_sum(out=lams[:].rearrange("p a l -> p (a l)"),
                                               in_=lamp[:].rearrange("p a (l d) -> p (a l) d", d=64), axis=AX.X),
                 reads=[lampb], writes=[lamsb])
            S.op("act", act(lams, lams, AF.Exp), reads=[lamsb], writes=[lamsb])
            S.op("dve", tt(nlam[:], lams[:, 1, :], lams[:, 0, :], ALU.subtract), reads=[lamsb], writes=[b_ada])
            lam_init = [0.8 - 0.6 * math.exp(-0.3 * l) for l in range(DEPTH)]
            for l in range(DEPTH):
                S.op("dve", ts(nlam[:, l:l + 1], nlam[:, l:l + 1], -lam_init[l], None, ALU.add),
                     reads=[b_ada], writes=[b_ada])
                S.op("dve", ts(g_sub[:, l:l + 1], g_sub[:, l:l + 1], 1.0 - lam_init[l], None, ALU.mult),
                     reads=[b_const, b_ada], writes=[b_ada])

            stage(1)
            def norm_mod(i, l, kind, pools, psrot):
                c0, N = BLK[i]
                wi = 0 if i == 0 else 1
                adat = ada_c if i == 0 else ada_l
                sh0 = 0 if kind == 0 else 24
                sqp, rsp, xrp = pools
                ps, psbuf = psrot.get()
                for k in range(KC):
                    sq, sqb = sqp.get()
                    eng = "act" if k % 2 == 0 else "pool"
                    if eng == "act":
                        S.op("act", act(sq[:, :N], xT[:, k, c0:c0 + N], AF.Square), reads=[b_x[i][k]], writes=[sqb])
                    else:
                        S.op("pool", tt(sq[:, :N], xT[:, k, c0:c0 + N], xT[:, k, c0:c0 + N], ALU.mult),
                             reads=[b_x[i][k]], writes=[sqb])
                    S.op("pe", mm(ps[:, :N], ones_bf[:], sq[:, :N], k == 0, k == KC - 1),
                         reads=[sqb, b_const], writes=[psbuf], inc=True)
                rs, rsb = rsp.get()
                S.op("act", act(rs[:, :N], ps[:, :N], AF.Ln, bias=EPS, scale=1.0 / D), reads=[psbuf], writes=[rsb])
                S.op("act", act(rs[:, :N], rs[:, :N], AF.Exp, scale=-0.5), reads=[rsb], writes=[rsb])
                for k in range(KC):
                    xr, xrb = xrp.get()
                    e1 = "dve" if k % 2 == 0 else "pool"
                    S.op(e1, tt(xr[:, :N], xT[:, k, c0:c0 + N], rs[:, :N], ALU.mult),
                         reads=[b_x[i][k], rsb], writes=[xrb])
                    S.op("act", act(aT[:, k, c0:c0 + N], xr[:, :N], AF.Identity,
                                    bias=adat[:, l, sh0 + k:sh0 + k + 1], scale=gc[:, wi, kind, l, k:k + 1]),
                         reads=[xrb, b_ada], writes=[b_a[i][k]])

            def load_w(dst, dstb, src_rows_ap):
                S.dma("pool", dm(dst, src_rows_ap), writes=[dstb])

            send_k = send.ap()[0:2176, :].rearrange("a b -> (a b)").rearrange("(h p t) -> h p t", h=4, p=128)
            send_v = send.ap()[2176:4352, :].rearrange("a b -> (a b)").rearrange("(h t d) -> t h d", h=4, d=128)
            send_u = send.ap()[4352:4384, :].rearrange("a b -> (a b)").rearrange("(g p c) -> p g c", g=4, p=128)

            for l in range(n_layers):
                last = (l == DEPTH - 1)
                phase()
                win, winb = aalloc([KC, 2048], BF16, "win")
                winbs = [Buf("win%d" % k).inherit(winb) for k in range(KC)]
                A["bufs"].extend(winbs)
                sqp = apool(3, [512], BF16, "sq")
                rsp = apool(1, [512], F32, "rs")
                xrp = apool(2, [512], F32, "xr")
                ropep = apool(2, [2, 512], F32, "rope")
                tqp = apool(2, [512], BF16, "tq")
                t12p = apool(3, [512], F32, "t12")
                stp = apool(3, [512], BF16, "stg")
                psrot = Rot(ps_all)
                for k in range(KC):
                    load_w(win[:, k, :], winbs[k], w_in_d[l, k * 128:(k + 1) * 128, :])
                for i in range(5):
                    norm_mod(i, l, 0, (sqp, rsp, xrp), psrot)
                for i in range(5):
                    c0, N = BLK[i]
                    if i > 0:
                        rp, rpb = ropep.get()
                        lc = c0 - 128
                        S.dma("sp", dm(rp[:, 0, :], cos_d[:, lc:lc + 512]), writes=[rpb])
                        S.dma("sp", dm(rp[:, 1, :], sin_d[:, lc:lc + 512]), writes=[rpb])
                    for m in range(12):
                        ps, psbuf = psrot.get()
                        for k in range(KC):
                            S.op("pe", mm(ps[:, :N], win[:, k, m * 128:(m + 1) * 128], aT[:, k, c0:c0 + N],
                                          k == 0, k == KC - 1),
                                 reads=[winbs[k], b_a[i][k]], writes=[psbuf], inc=(k == KC - 1))
                        if m < 4:
                            uc = ucol(c0)
                            S.op("act", act(uT[:, m, uc:uc + N], ps[:, :N], AF.Identity), reads=[psbuf], writes=[b_u[m][i]])
                            continue
                        isq = m < 8
                        hp = (m - 4) % 4
                        if isq:
                            dst, dstb = qT[:, hp, c0:c0 + N], b_q[i][hp]
                        else:
                            sg, sgb = stp.get()
                            dst, dstb = sg[:, :N], sgb
                        if i == 0:
                            S.op("act", act(dst, ps[:, :N], AF.Identity), reads=[psbuf], writes=[dstb])
                        else:
                            tq, tqb = tqp.get()
                            S.op("act", act(tq, ps[:, :N], AF.Identity), reads=[psbuf], writes=[tqb])
                            ps2, ps2b = psrot.get()
                            S.op("pe", mm(ps2[:, :N], pmat_bf[:], tq, True, True), reads=[tqb, b_const], writes=[ps2b])
                            t1, t1b = t12p.get()
                            S.op("pool", tt(t1, tq, rp[:, 0, :], ALU.mult), reads=[tqb, rpb], writes=[t1b])
                            t2, t2b = t12p.get()
                            S.op("dve", tt(t2, ps2[:, :N], rp[:, 1, :], ALU.mult), reads=[ps2b, rpb], writes=[t2b])
                            S.op("pool", tt(dst, t1, t2, ALU.add), reads=[t1b, t2b], writes=[dstb])
                        if not isq:
                            S.dma("sp", dm(send_k[hp, :, c0:c0 + N], dst), reads=[dstb], writes=[b_send_k[hp]])
                    for tt_ in range(N // 128):
                        ps, psbuf = psrot.get()
                        tc0 = c0 + tt_ * 128
                        for k in range(KC):
                            S.op("pe", mm(ps[:, :], aT[:, k, tc0:tc0 + 128], win[:, k, 1536:2048], k == 0, k == KC - 1),
                                 reads=[winbs[k], b_a[i][k]], writes=[psbuf], inc=(k == KC - 1))
                        sg, sgb = stp.get()
                        S.op("dve", cp(sg, ps[:, :]), reads=[psbuf], writes=[sgb])
                        S.dma("sp", dm(send_v[tc0:tc0 + 128, :, :], sg.rearrange("p (h d) -> p h d", d=128)),
                              reads=[sgb], writes=b_send_v)
                for reg, tl0 in enumerate((0, 120, 128, 2168)):
                    uc = ucol(tl0)
                    S.dma("sp", dm(send_u[:, :, reg * 8:(reg + 1) * 8], uT[:, :, uc:uc + 8]),
                          reads=[b_u[g][ii] for g in range(4) for ii in range(5)], writes=[b_send_u])
                stage(2)
                def gather(s0, n, rb, wb):
                    S.coll(lambda e: e.collective_compute(
                        "AllGather", ALU.bypass, replica_groups=[[0, 1], [2, 3], [4, 5], [6, 7]],
                        ins=[send.ap()[s0:s0 + n, :].opt()], outs=[recv.ap()[2 * s0:2 * s0 + 2 * n, :].opt()]),
                        reads=[rb], writes=[wb])
                gather(4352, 32, b_send_u, b_ru)
                for h in range(4):
                    gather(h * 544, 544, b_send_k[h], b_rk[h])
                    gather(2176 + h * 544, 544, b_send_v[h], b_rv[h])

                stage(3)
                phase()
                ktp = apool(2, [NKT * 128], BF16, "kt")
                vhp = apool(2, [NKT, 128], BF16, "vh")
                ptp = apool(4, [512], BF16, "pt")
                f32p = apool(6, [512], F32, "f32")
                hal, halb = aalloc([2, 4, 32], BF16, "hal")
                wpl, wplb = aalloc([4, 128], BF16, "wpl")
                pmp = apool(3, [528], F32, "pm")
                sp_, spb_ = aalloc([512], F32, "pS")
                yp = apool(4, [512], BF16, "y")
                pl2p = apool(2, [512], BF16, "pl2")
                etmp, etmpb = aalloc([8], F32, "etmp")
                blocks = [i for i in range(5) if not (last and i == 0)]

                def recv_piece(s0, n, r):
                    return recv.ap()[2 * s0 + r * n: 2 * s0 + (r + 1) * n, :].rearrange("a b -> (a b)")

                def load_kv(h):
                    kt, ktb = ktp.get()
                    vh, vhb = vhp.get()
                    for r in range(2):
                        S.dma("sp", dm(kt[:, r * TL:(r + 1) * TL],
                                       recv_piece(h * 544, 544, r).rearrange("(p t) -> p t", p=128)),
                              reads=[b_rk[h]], writes=[ktb])
                        S.dma("sp", dm(vh[:, r * NT:(r + 1) * NT, :],
                                       recv_piece(2176 + h * 544, 544, r).rearrange("(t p d) -> p t d", p=128, d=128)),
                              reads=[b_rv[h]], writes=[vhb])
                    return kt, ktb, vh, vhb

                kv = load_kv(0)
                for r in range(2):
                    S.dma("sp", dm(hal[:, r], recv_piece(4352, 32, r).rearrange("(g p c) -> p g c", g=4, p=128)),
                          reads=[b_ru], writes=[halb])
                S.dma("pool", dm(wpl, w_pool_d[l].rearrange("g c d -> c g d")), writes=[wplb])
                for (dcol, r, scol, mi) in ((0, 0, 8, 0), (136, 1, 0, 1), (144, 0, 24, 0), (2200, 1, 16, 1)):
                    S.op("dve", ts(uT[:, :, dcol:dcol + 8], hal[:, r, :, scol:scol + 8], msk[:, mi:mi + 1], None, ALU.mult),
                         reads=[halb, b_const], writes=[b_uh])

                ps_misc = Rot([ps_all[7]])

                def pool_ops(i):
                    c0, N = BLK[i]
                    uc = ucol(c0)
                    ys = []
                    for g in range(4):
                        rd = [b_u[g][ii] for ii in range(max(0, i - 1), min(5, i + 2))] + [b_uh]
                        w = 2 << g
                        h = w // 2
                        if g == 0:
                            Sv, Sb = sp_, spb_
                            S.op("pool", tt(Sv[:, :N], uT[:, 0, uc - 1:uc - 1 + N], uT[:, 0, uc:uc + N], ALU.add),
                                 reads=rd, writes=[Sb])
                        else:
                            n2 = N + 2 * h - 2
                            p2, p2b = pmp.get()
                            S.op("pool", tt(p2[:, :n2], uT[:, g, uc - h:uc - h + n2], uT[:, g, uc - h + 1:uc - h + 1 + n2], ALU.add),
                                 reads=rd, writes=[p2b])
                            cur, curb, span, ncur = p2, p2b, 2, n2
                            while span < h:
                                nn = ncur - span
                                nx, nxb = pmp.get()
                                S.op("pool", tt(nx[:, :nn], cur[:, 0:nn], cur[:, span:span + nn], ALU.add),
                                     reads=[curb], writes=[nxb])
                                cur, curb, span, ncur = nx, nxb, span * 2, nn
                            assert ncur == N + h, (ncur, N, h)
                            Sv, Sb = sp_, spb_
                            S.op("pool", tt(Sv[:, :N], cur[:, 0:N], cur[:, h:h + N], ALU.add), reads=[curb], writes=[Sb])
                        y, yb = yp.get()
                        S.op("dve", stt(y[:, :N], Sv[:, :N], 1.0 / w, uT[:, g, uc:uc + N], ALU.mult, ALU.subtract),
                             reads=[Sb] + rd, writes=[yb])
                        regs = {0: ((0, 0), (1, 120)), 1: ((2, 0),), 4: ((3, 504),)}.get(i, ())
                        for reg, col in regs:
                            S.op("pool", tt(etmp, Sv[:, col:col + 8], invc[:, g, reg, :], ALU.mult),
                                 reads=[Sb, b_const], writes=[etmpb])
                            S.op("pool", tt(y[:, col:col + 8], etmp, uT[:, g, uc + col:uc + col + 8], ALU.subtract),
                                 reads=[etmpb] + rd, writes=[yb])
                        ys.append((y, yb))
                    return ys

                def pool_mm(i, ys):
                    c0, N = BLK[i]
                    for g in range(4):
                        y, yb = ys[g]
                        ps, psbuf = ps_misc.get()
                        S.op("pe", mm(ps[:, :N], wpl[:, g, :], y[:, :N], True, True), reads=[wplb, yb], writes=[psbuf])
                        S.op("act", act(aT[:, g, c0:c0 + N], ps[:, :N], AF.Identity, scale=s_pool[:, l, g:g + 1]),
                             reads=[psbuf, b_const], writes=[b_a[i][g]])

                ys_next = pool_ops(blocks[0])

                stage(4)
                ps_s = Rot([ps_all[0], ps_all[1], ps_all[2]])
                ps_acc = [[ps_all[3], ps_all[4]], [ps_all[5], ps_all[6]]]
                for h in range(4):
                    kt, ktb, vh, vhb = kv
                    if h + 1 < 4:
                        kv = load_kv(h + 1)
                    for bi_, i in enumerate(blocks):
                        c0, N = BLK[i]
                        kts = [0, NT] if i == 0 else list(range(NKT))
                        for j in range(2):
                            (po, pob), (pl, plb) = ps_acc[j]
                            pend = None
                            lst = {"pendL": None, "prev": None}
                            assert len(kts) % 2 == 0
                            for n_, ktile in enumerate(kts):
                                pss, pssb = ps_s.get()
                                S.op("pe", mm(pss[:, :N], kt[64 * j:64 * j + 64, ktile * 128:(ktile + 1) * 128],
                                              qT[64 * j:64 * j + 64, h, c0:c0 + N], True, True),
                                     reads=[ktb, b_q[i][h]], writes=[pssb])
                                pt, ptb = ptp.get()
                                S.op("act", act(pt[:, :N], pss[:, :N], AF.Exp, scale=0.125), reads=[pssb], writes=[ptb])
                                if pend is not None:
                                    pend()

                                def pv(pt=pt, ptb=ptb, ktile=ktile, n_=n_, nk=len(kts),
                                       po=po, pob=pob, pl=pl, plb=plb, N=N, lst=lst):
                                    S.op("pe", mm(po[:, :N], vh[:, ktile, :], pt[:, :N], n_ == 0, n_ == nk - 1),
                                         reads=[vhb, ptb], writes=[pob], inc=True)
                                    if lst["pendL"] is not None:
                                        lst["pendL"]()
                                        lst["pendL"] = None
                                    if n_ % 2 == 0:
                                        lst["prev"] = (pt, ptb)
                                    else:
                                        ppt, pptb = lst["prev"]
                                        p2, p2b = pl2p.get()
                                        S.op("pool", tt(p2[:, :N], ppt[:, :N], pt[:, :N], ALU.add),
                                             reads=[pptb, ptb], writes=[p2b])

                                        def emitL(p2=p2, p2b=p2b, st_=(n_ == 1), sp2=(n_ == nk - 1)):
                                            S.op("pe", mm(pl[:, :N], ones_bf[:], p2[:, :N], st_, sp2),
                                                 reads=[p2b, b_const], writes=[plb], inc=True)
                                        lst["pendL"] = emitL
                                pend = pv
                            pend()
                            if lst["pendL"] is not None:
                                lst["pendL"]()
                                lst["pendL"] = None
                        (po0, pob0), (pl0, plb0) = ps_acc[0]
                        (po1, pob1), (pl1, plb1) = ps_acc[1]
                        r1, r1b = f32p.get()
                        r2, r2b = f32p.get()
                        o1, o1b = f32p.get()
                        o2, o2b = f32p.get()
                        S.op("dve", lambda e, o=r1[:, :N], i_=pl0[:, :N]: e.reciprocal(out=o, in_=i_), reads=[plb0], writes=[r1b])
                        S.op("dve", tt(o1[:, :N], po0[:, :N], r1[:, :N], ALU.mult), reads=[pob0, r1b], writes=[o1b])
                        S.op("dve", lambda e, o=r2[:, :N], i_=pl1[:, :N]: e.reciprocal(out=o, in_=i_), reads=[plb1], writes=[r2b])
                        S.op("dve", tt(o2[:, :N], po1[:, :N], r2[:, :N], ALU.mult), reads=[pob1, r2b], writes=[o2b])
                        S.op("dve", stt(o1[:, :N], o2[:, :N], nlam[:, l:l + 1], o1[:, :N], ALU.mult, ALU.add),
                             reads=[o2b, o1b, b_ada], writes=[o1b])
                        sq, sqb = ptp.get()
                        S.op("dve", tt(sq[:, :N], o1[:, :N], o1[:, :N], ALU.mult), reads=[o1b], writes=[sqb])
                        ps, psbuf = ps_misc.get()
                        S.op("pe", mm(ps[:, :N], ones_bf[:], sq[:, :N], True, True), reads=[sqb, b_const], writes=[psbuf])
                        S.op("act", act(r1[:, :N], ps[:, :N], AF.Ln, bias=EPS, scale=1.0 / 128), reads=[psbuf], writes=[r1b])
                        S.op("act", act(r1[:, :N], r1[:, :N], AF.Exp, scale=-0.5), reads=[r1b], writes=[r1b])
                        S.op("dve", stt(aT[:, 4 + h, c0:c0 + N], o1[:, :N], g_sub[:, l:l + 1], r1[:, :N], ALU.mult, ALU.mult),
                             reads=[o1b, r1b, b_ada], writes=[b_a[i][4 + h]])
                        if h == 0:
                            pool_mm(i, ys_next)
                            if bi_ + 1 < len(blocks):
                                ys_next = pool_ops(blocks[bi_ + 1])

                stage(5)
                phase()
                wo, wob = aalloc([KC, D], BF16, "wo")
                wobs = [Buf("wo%d" % k).inherit(wob) for k in range(KC)]
                A["bufs"].extend(wobs)
                for k in range(KC):
                    load_w(wo[:, k, :], wobs[k], w_out_d[l, k * 128:(k + 1) * 128, :])
                sqp = apool(2, [512], BF16, "sq")
                rsp = apool(1, [512], F32, "rs")
                xrp = apool(2, [512], F32, "xr")
                w1p = apool(2, [KC, 512], BF16, "w1")
                w2p = apool(2, [4, D], BF16, "w2")
                hidp = apool(2, [4, 512], BF16, "hid")
                rlp = apool(2, [512], BF16, "rl")
                psrot = Rot(ps_all)
                adat_of = lambda i: ada_c if i == 0 else ada_l

                def load_mlp(J):
                    w1, w1b = w1p.get()
                    w2, w2b = w2p.get()
                    for k in range(KC):
                        S.dma("pool", dm(w1[:, k, :], w1_d[l, k * 128:(k + 1) * 128, J * 512:(J + 1) * 512]), writes=[w1b])
                    for c in range(4):
                        S.dma("pool", dm(w2[:, c, :], w2_d[l, J * 512 + c * 128: J * 512 + (c + 1) * 128, :]), writes=[w2b])
                    return w1, w1b, w2, w2b

                for i in blocks:
                    c0, N = BLK[i]
                    for m in range(KC):
                        ps, psbuf = psrot.get()
                        for k in range(KC):
                            S.op("pe", mm(ps[:, :N], wo[:, k, m * 128:(m + 1) * 128], aT[:, k, c0:c0 + N], k == 0, k == KC - 1),
                                 reads=[wobs[k], b_a[i][k]], writes=[psbuf], inc=(k == KC - 1))
                        S.op("dve", stt(xT[:, m, c0:c0 + N], ps[:, :N], adat_of(i)[:, l, 16 + m:17 + m], xT[:, m, c0:c0 + N],
                                        ALU.mult, ALU.add),
                             reads=[psbuf, b_ada, b_x[i][m]], writes=[b_x[i][m]])
                mw = load_mlp(0)
                stage(6)
                for i in blocks:
                    norm_mod(i, l, 1, (sqp, rsp, xrp), psrot)
                stage(8)
                steps = [(J, i) for J in range(8) for i in blocks]
                wts = {0: mw}

                def mlp1(J, i):
                    c0, N = BLK[i]
                    w1, w1b, w2, w2b = wts[J]
                    hid, hidb = hidp.get()
                    for c in range(4):
                        ps, psbuf = psrot.get()
                        for k in range(KC):
                            S.op("pe", mm(ps[:, :N], w1[:, k, c * 128:(c + 1) * 128], aT[:, k, c0:c0 + N], k == 0, k == KC - 1),
                                 reads=[w1b, b_a[i][k]], writes=[psbuf], inc=(k == KC - 1))
                        rl, rlb = rlp.get()
                        S.op("act", act(rl[:, :N], ps[:, :N], AF.Relu), reads=[psbuf], writes=[rlb])
                        S.op("dve", tt(hid[:, c, :N], rl[:, :N], rl[:, :N], ALU.mult), reads=[rlb], writes=[hidb])
                    return hid, hidb

                def mlp2(J, i, hid, hidb):
                    c0, N = BLK[i]
                    w1, w1b, w2, w2b = wts[J]
                    for m in range(KC):
                        ps, psbuf = psrot.get()
                        for c in range(4):
                            S.op("pe", mm(ps[:, :N], w2[:, c, m * 128:(m + 1) * 128], hid[:, c, :N], c == 0, c == 3),
                                 reads=[w2b, hidb], writes=[psbuf], inc=(c == 3))
                        S.op("dve", stt(xT[:, m, c0:c0 + N], ps[:, :N], adat_of(i)[:, l, 40 + m:41 + m], xT[:, m, c0:c0 + N],
                                        ALU.mult, ALU.add),
                             reads=[psbuf, b_ada, b_x[i][m]], writes=[b_x[i][m]])

                cur_h = None
                for n_, (J, i) in enumerate(steps):
                    if i == blocks[0] and J + 1 < 8:
                        wts[J + 1] = load_mlp(J + 1)
                    if n_ == 0:
                        cur_h = mlp1(J, i)
                    nxt_h = None
                    if n_ + 1 < len(steps):
                        nxt_h = mlp1(*steps[n_ + 1])
                    mlp2(J, i, *cur_h)
                    cur_h = nxt_h

            stage(7)
            phase()
            sqp = apool(3, [512], BF16, "sq")
            rsp = apool(1, [512], F32, "rs")
            ynp = apool(3, [512], F32, "yn")
            otp = apool(2, [D], F32, "ot")
            psrot = Rot(ps_all)
            for i in range(1, 5):
                c0, N = BLK[i]
                ps, psbuf = psrot.get()
                for k in range(KC):
                    sq, sqb = sqp.get()
                    S.op("act", act(sq[:, :N], xT[:, k, c0:c0 + N], AF.Square), reads=[b_x[i][k]], writes=[sqb])
                    S.op("pe", mm(ps[:, :N], ones_bf[:], sq[:, :N], k == 0, k == KC - 1),
                         reads=[sqb, b_const], writes=[psbuf], inc=True)
                rs, rsb = rsp.get()
                S.op("act", act(rs[:, :N], ps[:, :N], AF.Ln, bias=EPS, scale=1.0 / D), reads=[psbuf], writes=[rsb])
                S.op("act", act(rs[:, :N], rs[:, :N], AF.Exp, scale=-0.5), reads=[rsb], writes=[rsb])
                yns = []
                for k in range(KC):
                    pass
                for tt_ in range(4):
                    ot, otb = otp.get()
                    for half in range(2):
                        ps2, ps2b = psrot.get()
                        for kk in range(4):
                            k = half * 4 + kk
                            yn, ynb = ynp.get()
                            tc0 = c0 + tt_ * 128
                            S.op("dve", stt(yn[:, :128], xT[:, k, tc0:tc0 + 128], g_fin[:, k:k + 1],
                                            rs[:, tt_ * 128:(tt_ + 1) * 128], ALU.mult, ALU.mult),
                                 reads=[b_x[i][k], rsb, b_const], writes=[ynb])
                            S.op("pe", (lambda o, i_: (lambda e: e.transpose(o, i_, ident[:])))(
                                ps2[:, kk * 128:(kk + 1) * 128], yn[:, :128]),
                                reads=[ynb, b_const], writes=[ps2b], inc=True)
                        S.op("act", (lambda o, i_: (lambda e: e.copy(out=o, in_=i_)))(ot[:, half * 512:(half + 1) * 512], ps2[:, :]),
                             reads=[ps2b], writes=[otb])
                    r0 = (i - 1) * 512 + tt_ * 128
                    S.dma("sp", dm(out_d[r0:r0 + 128, :], ot), reads=[otb], writes=[b_out])
        except _Stop:
            pass
        b_out.inherit(b_ru, b_send_u, *(b_rk + b_rv + b_send_k + b_send_v))
        if dump == "mix":
            S.dma("pool", dm(dump_d, aT[:]), reads=[b_a[i][k] for i in range(5) for k in range(KC)], writes=[b_out])
        elif dump is not None:
            S.dma("sp", dm(dump_d, xT[:]), reads=[b_x[i][k] for i in range(5) for k in range(KC)], writes=[b_out])
        S.wait_all("sp", [b_out])
        S.run(st)
    return nc


def _rope_tables(r):
    n = np.arange(2048 * r, 2048 * (r + 1))
    row = (n // 64).astype(np.float32)
    col = (n % 64).astype(np.float32)
    inv = (np.float32(10000.0) ** (-np.arange(0, 32, 2, dtype=np.float32) / np.float32(32))).astype(np.float32)
    ang = np.concatenate([row[:, None] * inv, col[:, None] * inv], axis=-1).astype(np.float32)
    cos = np.cos(ang).astype(np.float32).T
    sin = np.sin(ang).astype(np.float32).T
    return np.ascontiguousarray(np.tile(cos, (4, 1))), np.ascontiguousarray(np.tile(sin, (4, 1)))


def _pool_tables(r):
    invc = np.zeros((4, 4, 8), np.float32)
    for g in range(4):
        w = 2 << g
        for reg in range(4):
            n = 256 if reg < 2 else 4096
            half = n // 2
            for i in range(8):
                tl = i if reg % 2 == 0 else half - 8 + i
                t = r * half + tl
                lo = max(t - w // 2, 0)
                hi = min(t + w - w // 2, n)
                invc[g, reg, i] = 1.0 / float(hi - lo)
    msk = np.array([1.0, 0.0] if r == 1 else [0.0, 1.0], np.float32)
    return (np.ascontiguousarray(np.broadcast_to(invc, (128, 4, 4, 8))),
            np.ascontiguousarray(np.broadcast_to(msk, (128, 2))))


def make_in_maps(x, c, ctx, c_ctx, w_ada, b_ada, g_mix, g_mlp, w_in, w_pool, s_pool,
                 lam_q1, lam_k1, lam_q2, lam_k2, g_subln, w_out, w_mlp1, w_mlp2, g_final, n_layers=DEPTH):
    f = lambda a: np.ascontiguousarray(np.asarray(a, dtype=np.float32))
    x, c, ctx, c_ctx = f(x), f(c), f(ctx), f(c_ctx)
    w_ada = f(w_ada)
    w_in, w_pool, w_out, w_mlp1, w_mlp2 = [f(np.asarray(a)[:n_layers]) for a in (w_in, w_pool, w_out, w_mlp1, w_mlp2)]
    fm = lambda a, nch: f(np.asarray(a, np.float32).reshape(a.shape[0], nch, 128).transpose(2, 0, 1))
    b_adaT = fm(b_ada, 48)
    g_mixT = fm(g_mix, KC)
    g_mlpT = fm(g_mlp, KC)
    g_finT = f(np.asarray(g_final, np.float32).reshape(KC, 128).T)
    s_poolT = fm(s_pool, 4)
    g_subT = f(np.asarray(g_subln, np.float32).T)
    lam = np.stack([np.asarray(a, np.float32) for a in (lam_q1, lam_k1, lam_q2, lam_k2)], axis=0)
    lamv = f(np.broadcast_to(lam[None], (128, 4, DEPTH, 64)))
    ident = np.eye(128, dtype=np.float32)
    pmat = np.zeros((128, 128), np.float32)
    for m in range(128):
        if m % 64 < 32:
            pmat[m + 32, m] = -1.0
        else:
            pmat[m - 32, m] = 1.0
    maps = []
    for core in range(8):
        b, r = core // 2, core % 2
        x_loc = np.concatenate([ctx[b, 128 * r:128 * (r + 1)], x[b, 2048 * r:2048 * (r + 1)]], axis=0)
        c2 = np.stack([c_ctx, c[b]], axis=0)
        c2T = f(c2.reshape(2, KC, 128).transpose(2, 1, 0))
        cosT, sinT = _rope_tables(r)
        invc, msk = _pool_tables(r)
        maps.append({
            "x_loc": f(x_loc), "c2T": c2T,
            "w_ada_sl": f(w_ada[:, :, r * 3072:(r + 1) * 3072]),
            "b_adaT": b_adaT, "g_mixT": g_mixT, "g_mlpT": g_mlpT, "g_finT": g_finT,
            "s_poolT": s_poolT, "g_subT": g_subT, "lamv": lamv,
            "w_in": w_in, "w_pool": w_pool, "w_out": w_out, "w_mlp1": w_mlp1, "w_mlp2": w_mlp2,
            "cosT": cosT, "sinT": sinT, "invc": invc, "msk": msk, "ident": ident, "pmat": pmat,
        })
    return maps


_NC_CACHE = {}


def kernel(**inputs):
    maps = make_in_maps(**inputs)
    if "nc" not in _NC_CACHE:
        _NC_CACHE["nc"] = build_program()
    res = run_bass_kernel_spmd(_NC_CACHE["nc"], maps, core_ids=list(range(8)))
    out = np.empty((4, 4096, D), np.float32)
    for core in range(8):
        b, r = core // 2, core % 2
        out[b, 2048 * r:2048 * (r + 1)] = res.results[core]["out"]
    return out
```
